# Optimizing a Trainium2 kernel written in Bass

```python
import math
import jax, jax.numpy as jnp
from jax import lax
import numpy as np

D_MODEL = 1024
BATCH = 32
SEQ = 2048
DEPTH = 2

GRID_W = 64
CTX_LEN = 256

POOL_WIDTH = D_MODEL // 4
POOL_GROUPS = 4
POOL_WINDOWS = (2, 4, 8, 16)
HYENA_WIDTH = D_MODEL // 4
HYENA_SHORT = 3
HYENA_EMB = 33
HYENA_FILTER_ORDER = 64
HYENA_TARGET = 1e-2
HYENA_FAST_PCT = 0.3
HYENA_SLOW_PCT = 1.5
ATTN_WIDTH = D_MODEL // 2
HEAD_DIM = 64
ATTN_HEADS = ATTN_WIDTH // HEAD_DIM
ATTN_KV_HEADS = ATTN_HEADS // 4
ATTN_GROUP = ATTN_HEADS // ATTN_KV_HEADS
KV_WIDTH = ATTN_KV_HEADS * HEAD_DIM
Q_BLOCK = 128
ROPE_THETA = 10000.0
LN_EPS = 1e-6
QK_EPS = 1e-6
DEEPNORM_ALPHA = (2.0 * DEPTH) ** 0.25
DEEPNORM_BETA = (8.0 * DEPTH) ** -0.25

SPLIT_SIZES = (POOL_WIDTH, POOL_WIDTH, 3 * HYENA_WIDTH, HYENA_WIDTH, ATTN_WIDTH, KV_WIDTH, KV_WIDTH, ATTN_WIDTH)
IN_WIDTH = sum(SPLIT_SIZES)
SPLIT_POINTS = tuple(sum(SPLIT_SIZES[:i + 1]) for i in range(len(SPLIT_SIZES) - 1))
K_OFF = SPLIT_POINTS[4]
KV_END = SPLIT_POINTS[6]

kernel_name = "hybrid_pool_hyena_gqa_deepnorm_block"


def _layernorm(x):
    xf = x.astype(jnp.float32)
    mu = jnp.mean(xf, axis=-1, keepdims=True)
    var = jnp.mean(jnp.square(xf - mu), axis=-1, keepdims=True)
    return ((xf - mu) * lax.rsqrt(var + LN_EPS)).astype(x.dtype)


def _rms_head(x, g):
    xf = x.astype(jnp.float32)
    y = xf * lax.rsqrt(jnp.mean(xf * xf, axis=-1, keepdims=True) + QK_EPS) * g.astype(jnp.float32)
    return y.astype(x.dtype)


def _pool_mixer(v, pool_w, pool_scale):
    B, L, _ = v.shape
    gw = POOL_WIDTH // POOL_GROUPS
    vg = v.reshape(B, L, POOL_GROUPS, gw)
    cs = jnp.pad(jnp.cumsum(vg.astype(jnp.float32), axis=1), ((0, 0), (1, 0), (0, 0), (0, 0)))
    t = jnp.arange(L)
    means = []
    for g, w in enumerate(POOL_WINDOWS):
        lo = jnp.clip(t - w // 2, 0, L)
        hi = jnp.clip(t - w // 2 + w, 0, L)
        cnt = (hi - lo).astype(jnp.float32)[None, :, None]
        means.append((cs[:, hi, g] - cs[:, lo, g]) / cnt)
    mean = jnp.stack(means, axis=2).astype(v.dtype)
    y = jnp.einsum('blgc,gcd->blgd', mean - vg, pool_w).reshape(B, L, POOL_WIDTH)
    return y * pool_scale


def _short_conv(u, w, b):
    K, C = w.shape
    y = lax.conv_general_dilated(u, w[:, None, :].astype(u.dtype), window_strides=(1,),
                                 padding=[(K // 2, K // 2)], dimension_numbers=('NWC', 'WIO', 'NWC'),
                                 feature_group_count=C)
    return y + b


def _hyena_kernel(L, w1, b1, freq, w2, b2, w3):
    f32 = jnp.float32
    t = jnp.linspace(0.0, 1.0, L, dtype=f32)[:, None]
    bands = (HYENA_EMB - 1) // 2
    fr = jnp.linspace(1e-4, bands - 1, bands, dtype=f32)
    wpos = 2.0 * math.pi * jnp.arange(L, dtype=f32)[:, None] / L
    z = jnp.concatenate([t, jnp.cos(fr * wpos), -jnp.sin(fr * wpos)], axis=-1)
    fq = freq.astype(f32)
    h = jnp.sin(fq[0] * (z @ w1.astype(f32) + b1.astype(f32)))
    h = jnp.sin(fq[1] * (h @ w2.astype(f32) + b2.astype(f32)))
    h = (h @ w3.astype(f32)).reshape(L, 2, HYENA_WIDTH)
    max_decay = math.log(HYENA_TARGET) / HYENA_FAST_PCT
    min_decay = math.log(HYENA_TARGET) / HYENA_SLOW_PCT
    deltas = jnp.linspace(min_decay, max_decay, HYENA_WIDTH, dtype=f32)
    h = h * jnp.exp(-t * jnp.abs(deltas))[:, None, :]
    h_f, h_b = h[:, 0], h[:, 1]
    k = jnp.concatenate([h_f, jnp.zeros((1, HYENA_WIDTH), f32), h_b[:0:-1]], axis=0)
    return k / jnp.sum(jnp.abs(k), axis=0, keepdims=True)


def _bidir_fftconv(u, k, bias):
    L = u.shape[1]
    uf32 = u.astype(jnp.float32)
    uf = jnp.fft.rfft(uf32, n=2 * L, axis=1)
    kf = jnp.fft.rfft(k, n=2 * L, axis=0)
    y = jnp.fft.irfft(uf * kf[None], n=2 * L, axis=1)[:, :L]
    return (y + uf32 * bias.astype(jnp.float32)).astype(u.dtype)


def _hyena_mixer(uvx, conv_w, conv_b, w1, b1, freq, w2, b2, w3, bias):
    uvx = _short_conv(uvx, conv_w, conv_b)
    x0, x1, v = jnp.split(uvx, 3, axis=-1)
    k = _hyena_kernel(uvx.shape[1], w1, b1, freq, w2, b2, w3)
    return x0 * _bidir_fftconv(x1 * v, k, bias)


def _axial_rope(L):
    rows = L // GRID_W
    row = jnp.repeat(jnp.arange(rows), GRID_W)
    col = jnp.tile(jnp.arange(GRID_W), rows)
    axis_dim = HEAD_DIM // 2
    inv = ROPE_THETA ** (-jnp.arange(0, axis_dim, 2, dtype=jnp.float32) / axis_dim)
    ang = jnp.concatenate([row[:, None] * inv, col[:, None] * inv], axis=-1)
    return jnp.cos(ang), jnp.sin(ang)


def _apply_rope(x, cos, sin):
    B, L, H, D = x.shape
    xr = x.reshape(B, L, H, 2, 2, D // 4)
    a, b = xr[..., 0, :], xr[..., 1, :]
    c = cos.reshape(L, 1, 2, D // 4)
    s = sin.reshape(L, 1, 2, D // 4)
    return jnp.stack([a * c - b * s, a * s + b * c], axis=-2).reshape(B, L, H, D).astype(x.dtype)


def _sdpa(q, k, v):
    s = jnp.einsum('bqhgd,bkhd->bhgqk', q, k).astype(jnp.float32) * (HEAD_DIM ** -0.5)
    p = jax.nn.softmax(s, axis=-1).astype(v.dtype)
    return jnp.einsum('bhgqk,bkhd->bqhgd', p, v)


def _blocked_attention(q, k, v):
    B, L = q.shape[:2]
    nb = L // Q_BLOCK
    qb = jnp.moveaxis(q.reshape(B, nb, Q_BLOCK, *q.shape[2:]), 1, 0)
    ob = lax.map(lambda qq: _sdpa(qq, k, v), qb)
    return jnp.moveaxis(ob, 0, 1).reshape(q.shape)


def _heads_q(t, g, B, L):
    return _rms_head(t.reshape(B, L, ATTN_HEADS, HEAD_DIM), g)


def _heads_kv(t, B, L):
    return t.reshape(B, L, ATTN_KV_HEADS, HEAD_DIM)


def _merge(parts, attn_o, pool_w, pool_scale, hy_args, w_out):
    pool_v, pool_g, hy_u, hy_g, _, _, _, attn_g = parts
    y_pool = _pool_mixer(pool_v, pool_w, pool_scale) * jax.nn.silu(pool_g)
    y_hy = _hyena_mixer(hy_u, *hy_args) * jax.nn.silu(hy_g)
    y_attn = attn_o * jax.nn.silu(attn_g)
    return jnp.concatenate([y_pool, y_hy, y_attn], axis=-1) @ w_out


def setup_inputs(seed: int = 0) -> dict:
    key = jax.random.key(seed)
    ks = jax.random.split(key, 24)
    n = jax.random.normal
    f32 = jnp.float32
    D = D_MODEL
    return {
        "x": n(ks[0], (BATCH, SEQ, D), f32),
        "c": n(ks[1], (BATCH, D), f32),
        "ctx": n(ks[2], (BATCH, CTX_LEN, D), f32),
        "c_ctx": n(ks[3], (D,), f32),
        "w_ada": n(ks[4], (DEPTH, D, 3 * D), f32) * (0.5 * D ** -0.5),
        "b_ada": n(ks[5], (DEPTH, 3 * D), f32) * 0.01,
        "w_in": n(ks[6], (DEPTH, D, IN_WIDTH), f32) * D ** -0.5,
        "pool_w": n(ks[7], (DEPTH, POOL_GROUPS, POOL_WIDTH // POOL_GROUPS, POOL_WIDTH // POOL_GROUPS), f32) * (POOL_WIDTH // POOL_GROUPS) ** -0.5,
        "pool_scale": 1.0 + 0.02 * n(ks[8], (DEPTH, POOL_WIDTH), f32),
        "hy_conv_w": n(ks[9], (DEPTH, HYENA_SHORT, 3 * HYENA_WIDTH), f32) * HYENA_SHORT ** -0.5,
        "hy_conv_b": n(ks[10], (DEPTH, 3 * HYENA_WIDTH), f32) * 0.01,
        "hy_w1": n(ks[11], (DEPTH, HYENA_EMB, HYENA_FILTER_ORDER), f32) * HYENA_EMB ** -0.5,
        "hy_b1": n(ks[12], (DEPTH, HYENA_FILTER_ORDER), f32) * 0.01,
        "hy_freq": 1.0 + 0.02 * n(ks[13], (DEPTH, 2, HYENA_FILTER_ORDER), f32),
        "hy_w2": n(ks[14], (DEPTH, HYENA_FILTER_ORDER, HYENA_FILTER_ORDER), f32) * HYENA_FILTER_ORDER ** -0.5,
        "hy_b2": n(ks[15], (DEPTH, HYENA_FILTER_ORDER), f32) * 0.01,
        "hy_w3": n(ks[16], (DEPTH, HYENA_FILTER_ORDER, 2 * HYENA_WIDTH), f32) * HYENA_FILTER_ORDER ** -0.5,
        "hy_bias": n(ks[17], (DEPTH, HYENA_WIDTH), f32),
        "q_norm": 1.0 + 0.02 * n(ks[18], (DEPTH, HEAD_DIM), f32),
        "k_norm": 1.0 + 0.02 * n(ks[19], (DEPTH, HEAD_DIM), f32),
        "w_out": n(ks[20], (DEPTH, D, D), f32) * (D ** -0.5 * DEEPNORM_BETA),
        "ln_g": 1.0 + 0.02 * n(ks[21], (DEPTH, D), f32),
        "ln_b": n(ks[22], (DEPTH, D), f32) * 0.01,
    }


def reference(x, c, ctx, c_ctx, w_ada, b_ada, w_in, pool_w, pool_scale, hy_conv_w, hy_conv_b,
              hy_w1, hy_b1, hy_freq, hy_w2, hy_b2, hy_w3, hy_bias, q_norm, k_norm, w_out, ln_g, ln_b):
    B, L, _ = x.shape
    Lc = ctx.shape[1]
    cos, sin = _axial_rope(L)
    for i in range(DEPTH):
        last = i == DEPTH - 1
        mx = jax.nn.silu(c) @ w_ada[i] + b_ada[i]
        shift_x, scale_x, gate_x = jnp.split(mx[:, None, :], 3, axis=-1)
        mc = jax.nn.silu(c_ctx) @ w_ada[i] + b_ada[i]
        shift_c, scale_c, gate_c = jnp.split(mc, 3, axis=-1)
        hx = _layernorm(x) * (1.0 + scale_x) + shift_x
        hc = _layernorm(ctx) * (1.0 + scale_c) + shift_c
        hy_args = (hy_conv_w[i], hy_conv_b[i], hy_w1[i], hy_b1[i], hy_freq[i], hy_w2[i], hy_b2[i], hy_w3[i], hy_bias[i])

        if last:
            kc_raw, vc_raw = jnp.split(hc @ w_in[i][:, K_OFF:KV_END], [KV_WIDTH], axis=-1)
            pc = None
        else:
            pc = jnp.split(hc @ w_in[i], SPLIT_POINTS, axis=-1)
            kc_raw, vc_raw = pc[5], pc[6]
        kc = _rms_head(_heads_kv(kc_raw, B, Lc), k_norm[i])
        vc = _heads_kv(vc_raw, B, Lc)

        px = jnp.split(hx @ w_in[i], SPLIT_POINTS, axis=-1)
        qx = _apply_rope(_heads_q(px[4], q_norm[i], B, L), cos, sin).reshape(B, L, ATTN_KV_HEADS, ATTN_GROUP, HEAD_DIM)
        kx = _apply_rope(_rms_head(_heads_kv(px[5], B, L), k_norm[i]), cos, sin)
        vx = _heads_kv(px[6], B, L)
        k_all = jnp.concatenate([kx, kc], axis=1)
        v_all = jnp.concatenate([vx, vc], axis=1)
        ox = _blocked_attention(qx, k_all, v_all).reshape(B, L, ATTN_WIDTH)
        yx = _merge(px, ox, pool_w[i], pool_scale[i], hy_args, w_out[i])
        x_new = _layernorm(DEEPNORM_ALPHA * x + gate_x * yx) * ln_g[i] + ln_b[i]

        if not last:
            qc = _heads_q(pc[4], q_norm[i], B, Lc).reshape(B, Lc, ATTN_KV_HEADS, ATTN_GROUP, HEAD_DIM)
            oc = _sdpa(qc, kc, vc).reshape(B, Lc, ATTN_WIDTH)
            yc = _merge(pc, oc, pool_w[i], pool_scale[i], hy_args, w_out[i])
            ctx = _layernorm(DEEPNORM_ALPHA * ctx + gate_c * yc) * ln_g[i] + ln_b[i]
        x = x_new
    return x
```

```python
import math
from contextlib import ExitStack
import numpy as np
import ml_dtypes
import concourse.bass as bass
import concourse.mybir as mybir
from concourse.bass_utils import run_bass_kernel_spmd

F32 = mybir.dt.float32
BF16 = mybir.dt.bfloat16
AF = mybir.ActivationFunctionType
ALU = mybir.AluOpType
NPBF = ml_dtypes.bfloat16

D = 1024
KC = 8
L = 2048
LC = 256
DEPTH = 2
INW = 2816
NCH = 24
ALPHA = (2.0 * DEPTH) ** 0.25
LN_EPS = 1e-6
QK_EPS = 1e-6
MAGIC = 12582912.0
N_CORES = 8
INTERLEAVE = 100000
VERBOSE = False


def _dft_mats(Lh):
    N = 2 * Lh
    t = np.arange(Lh, dtype=np.float64)
    f = np.arange(Lh, dtype=np.float64)
    ang = 2.0 * np.pi * np.outer(t, f) / N
    Wf = np.zeros((Lh, N), np.float64)
    Wf[:, :Lh] = np.cos(ang)
    Wf[:, Lh:] = -np.sin(ang)
    Wf[:, Lh] = np.cos(np.pi * t)
    Wi = np.zeros((N, Lh), np.float64)
    cf = np.full(Lh, 2.0); cf[0] = 1.0
    Wi[:Lh, :] = (cf[:, None] / N) * np.cos(ang.T)
    Wi[Lh:, :] = -(2.0 / N) * np.sin(ang.T)
    Wi[Lh, :] = np.cos(np.pi * t) / N
    return Wf, Wi


def make_consts():
    c = {}
    c["ident_bf"] = np.eye(128, dtype=np.float32).astype(NPBF)
    c["ident_f"] = np.eye(128, dtype=np.float32)
    bo = (np.arange(128)[:, None] // 64 == np.arange(128)[None, :] // 64).astype(np.float32)
    c["bones"] = bo.astype(NPBF)
    c["ones_f"] = np.ones((128, 128), np.float32)
    P = np.zeros((128, 128), np.float32)
    for m in range(128):
        r = (m % 64) % 32
        if r < 16:
            P[m + 16, m] = -1.0
        else:
            P[m - 16, m] = 1.0
    c["rotP"] = P.astype(NPBF)
    t = np.arange(L)
    row = (t // 64).astype(np.float32); col = (t % 64).astype(np.float32)
    inv = (10000.0 ** (-np.arange(0, 32, 2, dtype=np.float32) / 32)).astype(np.float32)
    ang = np.concatenate([row[:, None] * inv, col[:, None] * inv], -1).astype(np.float32)
    idx = np.array([((p % 64) // 32) * 16 + (p % 16) for p in range(128)])
    c["ropeC"] = np.cos(ang)[:, idx].T.astype(NPBF).copy()
    c["ropeS"] = np.sin(ang)[:, idx].T.astype(NPBF).copy()
    wins = (2, 4, 8, 16)
    pcorr = np.ones((128, 2, 16), np.float32)
    for ci in range(2):
        for p in range(128):
            w = wins[2 * ci + p // 64]
            for i in range(8):
                if i < w // 2:
                    pcorr[p, ci, i] = w / (i + w // 2)
                pcorr[p, ci, 8 + i] = w / min(w, 8 - i + w // 2)
    c["pcorr"] = pcorr
    for nm, Lh in (("x", L), ("c", LC)):
        tt = np.linspace(0.0, 1.0, Lh, dtype=np.float32)[:, None]
        fr = np.linspace(1e-4, 15, 16, dtype=np.float32)
        wpos = (2.0 * math.pi * np.arange(Lh, dtype=np.float32)[:, None] / Lh).astype(np.float32)
        z = np.concatenate([tt, np.cos(fr * wpos), -np.sin(fr * wpos)], -1).astype(np.float32)
        c["zT_" + nm] = z.T.copy()
        max_decay = math.log(1e-2) / 0.3
        min_decay = math.log(1e-2) / 1.5
        deltas = np.linspace(min_decay, max_decay, 256, dtype=np.float32)
        dec = np.exp(-tt * np.abs(deltas)).astype(np.float32)
        dec2 = np.concatenate([dec, dec], 1)
        dec2[0, 256:] = 0.0
        c["dec_" + nm] = dec2.reshape(Lh // 128, 128, 512).transpose(1, 0, 2).copy()
        Wf, Wi = _dft_mats(Lh)
        NJ = 2 * Lh // 128; TC = Lh // 128
        c["Wf_" + nm] = Wf.reshape(TC, 128, NJ, 128).transpose(2, 1, 0, 3).astype(np.float32).astype(NPBF).copy()
        if nm == "x":
            c["Wi_x"] = Wi.reshape(NJ // 4, 4, 128, 4, 512).transpose(3, 0, 2, 1, 4).astype(np.float32).astype(NPBF).copy()
        else:
            c["Wi_c"] = Wi.reshape(NJ, 128, Lh).transpose(1, 0, 2).astype(np.float32).astype(NPBF).copy()
    return c


class PB:
    LIM = 30000

    def __init__(self, nc):
        self.nc = nc
        self.E = {"pe": nc.tensor, "act": nc.scalar, "dve": nc.vector, "pool": nc.gpsimd, "sp": nc.sync}
        self.sems = {}
        self.owner = {}
        self.seen = {e: {} for e in self.E}
        self.lw = {}
        self.rd = {}
        self.nsem = 0
        self.nops = 0

    def _sem(self, name):
        s = self.sems.get(name)
        if s is None or s[1] >= self.LIM:
            h = self.nc.alloc_semaphore("s%d" % self.nsem)
            self.nsem += 1
            s = [h, 0]
            self.sems[name] = s
            if name.startswith("E_"):
                self.owner[id(h)] = name[2:]
        return s

    def _wait(self, e, tok):
        h, cnt = tok
        if self.owner.get(id(h)) == e and e == "pe":
            return
        sn = self.seen[e]
        if sn.get(id(h), 0) >= cnt:
            return
        self.E[e].wait_ge(h, cnt)
        sn[id(h)] = cnt

    def op(self, e, fn, r=(), w=(), sig=True, dma=None):
        deps = []
        for k in r:
            t = self.lw.get(k)
            if t is not None:
                deps.append(t)
        for k in w:
            t = self.lw.get(k)
            if t is not None:
                deps.append(t)
            deps.extend(self.rd.get(k, ()))
        for t in deps:
            self._wait(e, t)
        ins = fn(self.E[e])
        self.nops += 1
        if dma is not None:
            s = self._sem(dma)
            s[1] += 16
            ins.then_inc(s[0], 16)
            tok = (s[0], s[1])
        else:
            s = self._sem("E_" + e)
            if sig:
                s[1] += 1
                ins.then_inc(s[0], 1)
                tok = (s[0], s[1])
            else:
                tok = (s[0], s[1] + 1)
        for k in w:
            self.lw[k] = tok
            self.rd[k] = []
        for k in r:
            self.rd.setdefault(k, []).append(tok)
        return tok

    def barrier(self):
        toks = [(s[0], s[1]) for n, s in self.sems.items() if s[1] > 0]
        for e in self.E:
            for t in toks:
                self._wait(e, t)

    def full_barrier(self):
        alltoks = [(s[0], s[1]) for s in self.sems.values() if s[1] > 0]
        for e in self.E:
            for t in alltoks:
                self._wait(e, t)
        self.lw.clear()
        self.rd.clear()


def pipeline(n, stages):
    S = len(stages)
    for t in range(n + S - 1):
        for si in range(S - 1, -1, -1):
            i = t - si
            if 0 <= i < n:
                stages[si](i)


class Ring:
    def __init__(self, name, aps):
        self.name = name
        self.aps = aps
        self.i = 0

    def next(self):
        i = self.i % len(self.aps)
        self.i += 1
        return self.aps[i], (self.name, i), "%s%d" % (self.name, i)


def build(NB, dbg=None):
    nc = bass.Bass("TRN2", target_bir_lowering=False)
    pb = PB(nc)
    consts = make_consts()

    def din(name, shape, dt=F32):
        return nc.dram_tensor(name, list(shape), dt, kind="ExternalInput").ap()

    x_d = din("x", [NB, L, D]); c_d = din("c", [NB, D]); ctx_d = din("ctx", [NB, LC, D]); cctx_d = din("c_ctx", [1, D])
    wada_d = din("w_ada", [2, D, 3 * D]); bada_d = din("b_ada", [2, 3 * D]); win_d = din("w_in", [2, D, INW])
    poolw_d = din("pool_w", [2, 4, 64, 64]); pscale_d = din("pool_scale", [2, 256])
    cw_d = din("hy_conv_w", [2, 3, 768]); cb_d = din("hy_conv_b", [2, 768])
    hw1_d = din("hy_w1", [2, 33, 64]); hb1_d = din("hy_b1", [2, 64]); hfq_d = din("hy_freq", [2, 2, 64])
    hw2_d = din("hy_w2", [2, 64, 64]); hb2_d = din("hy_b2", [2, 64]); hw3_d = din("hy_w3", [2, 64, 512])
    hbias_d = din("hy_bias", [2, 256]); qn_d = din("q_norm", [2, 64]); kn_d = din("k_norm", [2, 64])
    wout_d = din("w_out", [2, D, D]); lng_d = din("ln_g", [2, D]); lnb_d = din("ln_b", [2, D])
    cd = {}
    for k, v in consts.items():
        cd[k] = din("k_" + k, v.shape, BF16 if v.dtype == NPBF else F32)
    out_d = nc.dram_tensor("out", [NB, L, D], F32, kind="ExternalOutput").ap()
    winbf_d = nc.dram_tensor("winbf", [2, NCH, 128, KC, 128], BF16).ap()
    woutbf_d = nc.dram_tensor("woutbf", [2, 128, KC, D], BF16).ap()
    x1_d = nc.dram_tensor("x1s", [L, D], F32, **({"kind": "ExternalOutput"} if dbg else {})).ap()
    c1_d = nc.dram_tensor("c1s", [LC, D], F32, **({"kind": "ExternalOutput"} if dbg else {})).ap()
    kh_d = nc.dram_tensor("khs", [2, 16, 128, 2, 256], F32).ap()
    khc_d = nc.dram_tensor("khcs", [2, 128, 2, 256], F32).ap()

    es = ExitStack()

    uid = [0]

    def sb(name, shape, dt, stack=es):
        uid[0] += 1
        return stack.enter_context(nc.sbuf_tensor("%s_%d" % (name, uid[0]), list(shape), dt)).ap()

    NS = NB + 1
    ropeC = sb("ropeC", [128, L], BF16); ropeS = sb("ropeS", [128, L], BF16)
    ident_bf = sb("ident_bf", [128, 128], BF16); ident_f = sb("ident_f", [128, 128], F32)
    bones = sb("bones", [128, 128], BF16); rotP = sb("rotP", [128, 128], BF16)
    ones_f = sb("ones_f", [128, 128], F32); ones_bf = sb("ones_bf", [128, 64], BF16)
    pcorr = sb("pcorr", [128, 2, 16], F32)
    modT = sb("modT", [128, 2, 24, NS], F32)
    cwT = sb("cwT", [128, 2, 6, 3], F32); cbT = sb("cbT", [128, 2, 6], F32)
    hbT = sb("hbT", [128, 2, 2], F32); psT = sb("psT", [128, 2, 2], F32)
    gq = sb("gq", [128, 2], F32); gk = sb("gk", [128, 2], F32)
    pwblk = sb("pwblk", [128, 2, 2, 128], BF16)
    epsc = sb("epsc", [128, 4], F32)
    psum_all = es.enter_context(nc.psum_tensor("psall", [128, 4096], F32)).ap()
    psum = [psum_all[:, i * 512:(i + 1) * 512] for i in range(8)]

    class PS:
        def __init__(self):
            self.i = 0

        def next(self, lo=0, hi=8):
            n = hi - lo
            i = lo + (self.i % n)
            self.i += 1
            return psum[i], ("ps", i)

    psr = PS()
    SP, PE, ACT, DVE, POOL = "sp", "pe", "act", "dve", "pool"
    op = pb.op

    def dma(q, out, in_, r, w, sem, slow=False):
        return op(q, lambda e: e.dma_start(out=out, in_=in_, allow_slow_non_contiguous=slow) if slow else e.dma_start(out=out, in_=in_),
                  r=r, w=w, dma=sem)

    dumps = {}

    def dump(name, ap, keys):
        if not dbg or name not in dbg or name in dumps:
            return
        t = nc.dram_tensor("dbg_" + name, list(ap.shape), ap.dtype, kind="ExternalOutput").ap()
        dumps[name] = t
        dma(SP, t, ap, keys, [("dbg", name)], "dbg_" + name)

    cl = [(ident_bf, cd["ident_bf"]), (ident_f, cd["ident_f"]), (bones, cd["bones"]), (rotP, cd["rotP"]),
          (ones_f, cd["ones_f"]), (ropeC, cd["ropeC"]), (ropeS, cd["ropeS"]), (pcorr, cd["pcorr"])]
    for dst, src in cl:
        dma(SP, dst, src, (), (), "const")
    for l in range(2):
        for tp in range(3):
            dma(SP, cwT[:, l, :, tp], cw_d[l, tp:tp + 1, :].rearrange("o (m p) -> p (o m)", p=128), (), (), "const", slow=True)
        dma(SP, cbT[:, l, :], cb_d[l:l + 1, :].rearrange("o (m p) -> p (o m)", p=128), (), (), "const", slow=True)
        dma(SP, hbT[:, l, :], hbias_d[l:l + 1, :].rearrange("o (m p) -> p (o m)", p=128), (), (), "const", slow=True)
        dma(SP, psT[:, l, :], pscale_d[l:l + 1, :].rearrange("o (m p) -> p (o m)", p=128), (), (), "const", slow=True)
        for hf in range(2):
            dma(SP, gq[hf * 64:(hf + 1) * 64, l:l + 1], qn_d[l:l + 1, :].rearrange("o d -> d o"), (), (), "const", slow=True)
            dma(SP, gk[hf * 64:(hf + 1) * 64, l:l + 1], kn_d[l:l + 1, :].rearrange("o d -> d o"), (), (), "const", slow=True)
    op(DVE, lambda e: e.memset(ones_bf, 1.0), w=["ones_bf"])
    PS_HALF = True
    op(DVE, lambda e: e.memset(epsc[:, 0:1], LN_EPS), w=["epsc"])
    op(DVE, lambda e: e.memset(epsc[:, 1:2], LN_EPS / (ALPHA * ALPHA)), w=["epsc"])
    op(DVE, lambda e: e.memset(epsc[:, 2:3], QK_EPS), w=["epsc"])
    op(DVE, lambda e: e.memset(epsc[:, 3:4], 0.0), w=["epsc"])
    pb.full_barrier()

    op(DVE, lambda e: e.tensor_scalar(out=psT, in0=psT, scalar1=0.5, scalar2=None, op0=ALU.mult), w=["psT"])
    pb.full_barrier()

    cg_stack = ExitStack()

    def conv_gen():
        ph = cg_stack
        if True:
            stg = Ring("stg", [sb("stg%d" % i, [128, INW], F32, ph) for i in range(2)])
            stb = Ring("stb", [sb("stb%d" % i, [128, INW + 256], BF16, ph) for i in range(2)])
            cnt = 0
            pwfs = [sb("pwf%d" % l, [128, 2, 128], F32, ph) for l in range(2)]
            for l in range(2):
                pwf = pwfs[l]
                op(DVE, lambda e: e.memset(pwf, 0.0), w=[("pwf", l)])
                for g in range(4):
                    ci, hf = g // 2, g % 2
                    dma(SP, pwf[hf * 64:(hf + 1) * 64, ci, hf * 64:(hf + 1) * 64], poolw_d[l, g], (), [("pwf", l)], "pw")
                op(DVE, lambda e: e.tensor_copy(out=pwblk[:, l, :, :], in_=pwf), r=[("pwf", l)], w=["pwblk"])
                for k in range(KC):
                    st, sk, ss = stg.next(); bt, bk, bs = stb.next()
                    dma(SP, st, win_d[l, k * 128:(k + 1) * 128, :], (), [sk], ss)
                    eng = DVE if cnt % 2 == 0 else ACT
                    cnt += 1
                    if eng == DVE:
                        op(DVE, lambda e: e.tensor_copy(out=bt[:, 0:INW], in_=st), r=[sk], w=[bk])
                        op(DVE, lambda e: e.tensor_copy(out=bt[:, INW:INW + 256].rearrange("p (h r d) -> p h r d", h=2, r=2),
                                                        in_=st[:, 2048:2176].rearrange("p (h d) -> p h d", h=2).unsqueeze(2).to_broadcast([128, 2, 2, 64])),
                           r=[sk], w=[bk])
                    else:
                        op(ACT, lambda e: e.copy(out=bt[:, 0:INW], in_=st), r=[sk], w=[bk])
                        op(ACT, lambda e: e.copy(out=bt[:, INW:INW + 256].rearrange("p (h r d) -> p h r d", h=2, r=2),
                                                 in_=st[:, 2048:2176].rearrange("p (h d) -> p h d", h=2).unsqueeze(2).to_broadcast([128, 2, 2, 64])),
                           r=[sk], w=[bk])
                    dma(POOL, winbf_d[l, :, :, k, :].rearrange("m p c -> p m c"), bt.rearrange("p (m c) -> p m c", c=128), [bk], [], bs)
                    yield
                for k in range(KC):
                    st, sk, ss = stg.next(); bt, bk, bs = stb.next()
                    dma(SP, st[:, 0:D], wout_d[l, k * 128:(k + 1) * 128, :], (), [sk], ss)
                    op(DVE, lambda e: e.tensor_copy(out=bt[:, 0:D], in_=st[:, 0:D]), r=[sk], w=[bk])
                    dma(POOL, woutbf_d[l, :, k, :], bt[:, 0:D], [bk], [], bs)
                    yield


    cg = conv_gen()
    next(cg, None)

    with ExitStack() as ph:
        scT = sb("scT", [128, KC, NS], F32, ph)
        badaT = sb("badaT", [128, 2, 24], F32, ph)
        wab = Ring("wab", [sb("wab%d" % i, [128, KC, 512], F32, ph) for i in range(2)])
        for b in range(NB):
            dma(SP, scT[:, :, b:b + 1], c_d[b:b + 1, :].rearrange("o (k p) -> p k o", p=128), (), ["scT"], "m_sc", slow=True)
        dma(SP, scT[:, :, NB:NB + 1], cctx_d.rearrange("o (k p) -> p k o", p=128), (), ["scT"], "m_sc", slow=True)
        for l in range(2):
            dma(SP, badaT[:, l, :], bada_d[l:l + 1, :].rearrange("o (j p) -> p (o j)", p=128), (), ["badaT"], "m_ba", slow=True)
        op(ACT, lambda e: e.activation(out=scT, in_=scT, func=AF.Silu), r=["scT"], w=["scT"])
        op(DVE, lambda e: e.tensor_scalar(out=badaT[:, :, 8:16], in0=badaT[:, :, 8:16], scalar1=1.0, scalar2=None, op0=ALU.add),
           r=["badaT"], w=["badaT"])
        for l in range(2):
            for jb in range(6):
                wt, wk, ws = wab.next()
                dma(SP, wt, wada_d[l, :, jb * 512:(jb + 1) * 512].rearrange("(k p) n -> p k n", p=128), (), [wk], ws)
                for jj in range(4):
                    j = jb * 4 + jj
                    pt, pk = psr.next()
                    for k in range(KC):
                        op(PE, lambda e: e.matmul(pt[:, 0:NS], lhsT=wt[:, k, jj * 128:(jj + 1) * 128], rhs=scT[:, k, :], start=(k == 0), stop=(k == KC - 1)),
                           r=[wk, "scT"], w=[pk], sig=(k == KC - 1))
                    op(DVE, lambda e: e.tensor_scalar(out=modT[:, l, j, :], in0=pt[:, 0:NS], scalar1=badaT[:, l, j:j + 1], scalar2=None, op0=ALU.add),
                       r=[pk, "badaT"], w=["modT"])
        dump("modT", modT, ["modT"])
        pb.barrier()

    def hyena_filter(l, nm, Lh):
        TC = Lh // 128; NJ = 2 * TC; NP_ = TC; BWf = min(512, Lh); NBk = Lh // BWf
        with ExitStack() as ph:
            zT = sb("zT", [33, Lh], F32, ph)
            w1 = sb("w1", [33, 64], F32, ph); w2 = sb("w2", [64, 64], F32, ph); w3 = sb("w3", [64, 512], F32, ph)
            prm = sb("prm", [64, 8], F32, ph)
            h1 = sb("h1", [64, Lh], F32, ph); h2 = sb("h2", [64, Lh], F32, ph)
            ty = sb("ty", [64, 512], F32, ph); tr = sb("tr", [64, 512], F32, ph)
            hn = sb("hn", [128, TC, 512], BF16, ph)
            hdr = Ring("hdr", [sb("hdr%d" % i, [128, 512], F32, ph) for i in range(2)])
            dcr = Ring("dcr", [sb("dcr%d" % i, [128, 512], F32, ph) for i in range(2)])
            ha = sb("ha", [128, 512], F32, ph); nrm = sb("nrm", [128, 256], F32, ph); rinv = sb("rinv", [128, 256], F32, ph)
            qs = sb("qs", [128, 256], F32, ph); kt = Ring("kt", [sb("kt%d" % i, [128, 256], F32, ph) for i in range(2)])
            wfr = Ring("wff", [sb("wff%d" % i, [128, TC, 128], BF16, ph) for i in range(3)])
            ld = [(zT, cd["zT_" + nm]), (w1, hw1_d[l]), (w2, hw2_d[l]), (w3, hw3_d[l])]
            for dst, src in ld:
                dma(SP, dst, src, (), ["hf_in"], "hfl")
            dma(SP, prm[:, 0:1], hb1_d[l:l + 1, :].rearrange("o d -> d o"), (), ["hf_in"], "hfl", slow=True)
            dma(SP, prm[:, 1:2], hb2_d[l:l + 1, :].rearrange("o d -> d o"), (), ["hf_in"], "hfl", slow=True)
            dma(SP, prm[:, 2:4], hfq_d[l].rearrange("t d -> d t"), (), ["hf_in"], "hfl", slow=True)
            hs = pb.sems["hfl"]
            inv2pi = 1.0 / (2.0 * math.pi)
            op(DVE, lambda e: e.tensor_scalar(out=prm[:, 4:5], in0=prm[:, 2:3], scalar1=inv2pi, scalar2=None, op0=ALU.mult), r=["hf_in"], w=["prm"])
            op(DVE, lambda e: e.tensor_scalar(out=prm[:, 6:7], in0=prm[:, 3:4], scalar1=inv2pi, scalar2=None, op0=ALU.mult), w=["prm"])
            op(DVE, lambda e: e.tensor_tensor(out=prm[:, 5:6], in0=prm[:, 4:5], in1=prm[:, 0:1], op=ALU.mult), w=["prm"])
            op(DVE, lambda e: e.tensor_tensor(out=prm[:, 7:8], in0=prm[:, 6:7], in1=prm[:, 1:2], op=ALU.mult), w=["prm"])
            op(PE, lambda e: e.wait_ge(hs[0], hs[1]), r=["hf_in"], sig=False)

            def sin_layer(src_w, src_rhs, kdim, acol, bcol, dst):
                for n in range(NBk):
                    pt, pk = psr.next()
                    op(PE, lambda e: e.matmul(pt[0:64, 0:BWf], lhsT=src_w, rhs=src_rhs[0:kdim, n * BWf:(n + 1) * BWf], start=True, stop=True),
                       r=["prm", ("hsrc", kdim)], w=[pk])
                    op(DVE, lambda e: e.tensor_scalar(out=ty[:, 0:BWf], in0=pt[0:64, 0:BWf], scalar1=prm[:, acol:acol + 1], scalar2=prm[:, bcol:bcol + 1], op0=ALU.mult, op1=ALU.add),
                       r=[pk, "prm"], w=["ty"])
                    op(DVE, lambda e: e.tensor_scalar(out=tr[:, 0:BWf], in0=ty[:, 0:BWf], scalar1=MAGIC, scalar2=None, op0=ALU.add), r=["ty"], w=["tr"])
                    op(DVE, lambda e: e.tensor_scalar(out=tr[:, 0:BWf], in0=tr[:, 0:BWf], scalar1=MAGIC, scalar2=None, op0=ALU.subtract), r=["tr"], w=["tr"])
                    op(DVE, lambda e: e.tensor_tensor(out=ty[:, 0:BWf], in0=ty[:, 0:BWf], in1=tr[:, 0:BWf], op=ALU.subtract), r=["ty", "tr"], w=["ty"])
                    op(ACT, lambda e: e.activation(out=dst[:, n * BWf:(n + 1) * BWf], in_=ty[:, 0:BWf], func=AF.Sin, scale=6.283185),
                       r=["ty"], w=[("hsrc", 64)])

            sin_layer(w1, zT, 33, 4, 5, h1)
            sin_layer(w2, h1, 64, 6, 7, h2)
            st_, sk_ = psr.next()
            for tc in range(TC):
                pt, pk = psr.next(0, 6)
                op(PE, lambda e: e.matmul(pt, lhsT=h2[:, tc * 128:(tc + 1) * 128], rhs=w3, start=True, stop=True), r=[("hsrc", 64)], w=[pk])
                dt_, dtk, dts = dcr.next()
                dma(SP, dt_, cd["dec_" + nm][:, tc, :], (), [dtk], dts)
                hd, hdk, _ = hdr.next()
                op(DVE, lambda e: e.tensor_tensor(out=hd, in0=pt, in1=dt_, op=ALU.mult), r=[pk, dtk], w=[hdk])
                op(ACT, lambda e: e.activation(out=ha, in_=hd, func=AF.Abs), r=[hdk], w=["ha"])
                op(DVE, lambda e: e.tensor_copy(out=hn[:, tc, :], in_=hd), r=[hdk], w=["hn"])
                op(PE, lambda e: e.matmul(psum[7], lhsT=ones_f, rhs=ha, start=(tc == 0), stop=(tc == TC - 1)), r=["ha"], w=[("ps", 7)])
            op(ACT, lambda e: e.copy(out=qs, in_=psum[7][:, 256:512]), r=[("ps", 7)], w=["qs"])
            op(DVE, lambda e: e.tensor_tensor(out=nrm, in0=psum[7][:, 0:256], in1=qs, op=ALU.add), r=[("ps", 7), "qs"], w=["nrm"])
            op(DVE, lambda e: e.reciprocal(out=rinv, in_=nrm), r=["nrm"], w=["rinv"])
            for j in range(NJ):
                next(cg, None)
                wt, wk, ws = wfr.next()
                dma(SP, wt, cd["Wf_" + nm][j], (), [wk], ws)
                pt, pk = psr.next(0, 6)
                for tc in range(TC):
                    op(PE, lambda e: e.matmul(pt, lhsT=wt[:, tc, :], rhs=hn[:, tc, :], start=(tc == 0), stop=(tc == TC - 1)), r=[wk, "hn"], w=[pk], sig=(tc == TC - 1))
                op(ACT, lambda e: e.copy(out=qs, in_=pt[:, 256:512]), r=[pk], w=["qs"])
                jp, comp = j % NP_, j // NP_
                dst, ktk, kts = kt.next(); dk = ktk
                op(DVE, lambda e: e.tensor_tensor(out=dst, in0=pt[:, 0:256], in1=qs, op=(ALU.add if comp == 0 else ALU.subtract)), r=[pk, "qs"], w=[dk])
                if comp == 1 and jp == 0:
                    op(DVE, lambda e: e.tensor_tensor(out=dst[0:1, :], in0=pt[0:1, 0:256], in1=qs[0:1, :], op=ALU.add), r=[pk, "qs"], w=[dk])
                op(DVE, lambda e: e.tensor_tensor(out=dst, in0=dst, in1=rinv, op=ALU.mult), r=[dk, "rinv"], w=[dk])
                dma(POOL, (kh_d[l] if nm == "x" else khc_d)[jp, :, comp, :], dst, [dk], [("kh", l)], kts)
        pb.full_barrier()

    hyena_filter(0, "x", L)
    hyena_filter(1, "x", L)
    hyena_filter(0, "c", LC)
    for _ in cg:
        pass
    pb.full_barrier()
    cg_stack.close()

    hxT = sb("hxT", [128, KC, L], BF16)
    cat = sb("cat", [128, KC, L], BF16)
    kTc = sb("kTc", [128, 2, LC], BF16)
    vcx = sb("vcx", [128, 2, 128], BF16)
    wring = Ring("wi", [sb("wslot%d" % i, [128, 2, KC, 128], BF16) for i in range(2)])

    def load_w(l, m0, n=2):
        wt, wk, ws = wring.next()
        dma(SP, wt[:, 0:n, :, :], winbf_d[l, m0:m0 + n].rearrange("m p k c -> p m k c"), (), [wk], ws)
        return wt, wk

    def proj(wt, wk, mi, n, BW, ps_lo=0, ps_hi=8):
        pt, pk = psr.next(ps_lo, ps_hi)
        for k in range(KC):
            op(PE, lambda e: e.matmul(pt[:, 0:BW], lhsT=wt[:, mi, k, :], rhs=hxT[:, k, n * BW:(n + 1) * BW], start=(k == 0), stop=(k == KC - 1)),
               r=[wk, "hxT"], w=[pk], sig=(k == KC - 1))
        return pt, pk

    def seq_block(l, b, Ls, src_d, dst_d, is_ctx, kv_only, dkey_src, dkey_dst):
        NT = Ls // 128; BW = min(512, Ls); NBk = Ls // BW
        col = NB if is_ctx else b
        last = (l == DEPTH - 1)
        G = min(4, NT); NG = NT // G
        psb = psum_all.bitcast(BF16).rearrange("p (b k c) -> p b k c", b=8, k=8)
        with ExitStack() as ph:
            xg = [sb("A_x%d" % i, [128, G, D], F32, ph) for i in range(2)]
            xng = [sb("A_xn%d" % i, [128, G, D], BF16, ph) for i in range(2)]
            sts = [sb("A_st%d" % i, [128, G, 12], F32, ph) for i in range(3)]
            mvs = [sb("A_mv%d" % i, [128, G, 2], F32, ph) for i in range(3)]
            rss = [sb("A_rs%d" % i, [128, G, 2], F32, ph) for i in range(3)]

            def a1_(g):
                xt = xg[g % 2]; st = sts[g % 3]; mv = mvs[g % 3]
                dma(SP, xt, src_d[g * G * 128:(g + 1) * G * 128, :].rearrange("(a p) d -> p a d", p=128), [(dkey_src, g * G + a) for a in range(G)], [("A_x", g % 2)], "A_x%d" % (g % 2))
                for a in range(G):
                    for hh in range(2):
                        op(DVE, lambda e: e.bn_stats(out=st[:, a, hh * 6:(hh + 1) * 6], in_=xt[:, a, hh * 512:(hh + 1) * 512]), r=[("A_x", g % 2)], w=[("A_st", g % 3)])
                    op(DVE, lambda e: e.bn_aggr(out=mv[:, a, :], in_=st[:, a, :]), r=[("A_st", g % 3)], w=[("A_mv", g % 3)])

            def a2_(g):
                mv = mvs[g % 3]; rs = rss[g % 3]
                op(ACT, lambda e: e.activation(out=rs[:, :, 0], in_=mv[:, :, 1], func=AF.Sqrt, bias=epsc[:, 0:1], scale=1.0), r=[("A_mv", g % 3)], w=[("A_rs0", g % 3)])

            def a3_(g):
                xt = xg[g % 2]; mv = mvs[g % 3]; rs = rss[g % 3]; xn_ = xng[g % 2]
                op(DVE, lambda e: e.reciprocal(out=rs[:, :, 1], in_=rs[:, :, 0]), r=[("A_rs0", g % 3)], w=[("A_rs1", g % 3)])
                for a in range(G):
                    op(DVE, lambda e: e.tensor_scalar(out=xn_[:, a, :], in0=xt[:, a, :], scalar1=mv[:, a, 0:1], scalar2=rs[:, a, 1:2], op0=ALU.subtract, op1=ALU.mult),
                       r=[("A_x", g % 2), ("A_mv", g % 3), ("A_rs1", g % 3)], w=[("A_xn", g % 2)])

            def a4_(g):
                xn_ = xng[g % 2]
                b0 = (g % 2) * 4
                for a in range(G):
                    for k in range(KC):
                        op(PE, lambda e: e.transpose(psb[:, b0 + a, k, :], xn_[:, a, k * 128:(k + 1) * 128], ident_bf), r=[("A_xn", g % 2)], w=[("ps", b0 + a)], sig=(k == KC - 1))

            def a5_(g):
                b0 = (g % 2) * 4
                for k in range(KC):
                    dst = hxT[:, k, g * G * 128:(g + 1) * G * 128].rearrange("p (a c) -> p a c", c=128)
                    src = psb[:, b0:b0 + G, k, :]
                    rk = [("ps", b0 + a) for a in range(G)]
                    if k % 2 == 0:
                        op(ACT, lambda e: e.activation(out=dst, in_=src, func=AF.Identity, bias=modT[:, l, k, col:col + 1], scale=modT[:, l, 8 + k, col:col + 1]), r=rk, w=["hxT"])
                    else:
                        op(DVE, lambda e: e.tensor_scalar(out=dst, in0=src, scalar1=modT[:, l, 8 + k, col:col + 1], scalar2=modT[:, l, k, col:col + 1], op0=ALU.mult, op1=ALU.add), r=rk, w=["hxT"])

            pipeline(NG, [a1_, a2_, a3_, a4_, a5_])
        dump("hxT_%d_%d_%d" % (l, b, int(is_ctx)), hxT[:, :, 0:Ls], ["hxT"])
        pb.barrier()

        NK = Ls if is_ctx else Ls + LC
        NKC = NK // 128
        OLO, OHI = 6, 8
        ALO, AHI = 2, 6

        def normrope(pt, pk, gcol, dst, dkey, n, rope, tmp, plo, phi):
            sq, qg, rstd, t1, t2 = tmp
            op(ACT, lambda e: e.activation(out=sq[:, 0:BW], in_=pt[:, 0:BW], func=AF.Square), r=[pk], w=["nr_sq"])
            op(ACT, lambda e: e.activation(out=qg[:, 0:BW], in_=pt[:, 0:BW], func=AF.Identity, scale=gcol), r=[pk], w=["nr_qg"])
            p2, k2 = psr.next(plo, phi)
            op(PE, lambda e: e.matmul(p2[:, 0:BW], lhsT=bones, rhs=sq[:, 0:BW], start=True, stop=True), r=["nr_sq"], w=[k2])
            op(ACT, lambda e: e.activation(out=rstd[:, 0:BW], in_=p2[:, 0:BW], func=AF.Ln, bias=epsc[:, 2:3], scale=1.0 / 64), r=[k2], w=["nr_rstd"])
            op(ACT, lambda e: e.activation(out=rstd[:, 0:BW], in_=rstd[:, 0:BW], func=AF.Exp, scale=-0.5), r=["nr_rstd"], w=["nr_rstd"])
            if rope:
                p3, k3 = psr.next(plo, phi)
                op(PE, lambda e: e.matmul(p3[:, 0:BW], lhsT=rotP, rhs=qg[:, 0:BW], start=True, stop=True), r=["nr_qg"], w=[k3])
                op(DVE, lambda e: e.tensor_tensor(out=t1[:, 0:BW], in0=qg[:, 0:BW], in1=ropeC[:, n * BW:(n + 1) * BW], op=ALU.mult), r=["nr_qg"], w=["nr_t1"])
                op(DVE, lambda e: e.tensor_tensor(out=t2[:, 0:BW], in0=p3[:, 0:BW], in1=ropeS[:, n * BW:(n + 1) * BW], op=ALU.mult), r=[k3], w=["nr_t2"])
                op(DVE, lambda e: e.tensor_tensor(out=t1[:, 0:BW], in0=t1[:, 0:BW], in1=t2[:, 0:BW], op=ALU.add), r=["nr_t1", "nr_t2"], w=["nr_t1"])
                op(DVE, lambda e: e.tensor_tensor(out=dst, in0=t1[:, 0:BW], in1=rstd[:, 0:BW], op=ALU.mult), r=["nr_t1", "nr_rstd"], w=[dkey])
            else:
                op(DVE, lambda e: e.tensor_tensor(out=dst, in0=qg[:, 0:BW], in1=rstd[:, 0:BW], op=ALU.mult), r=["nr_qg", "nr_rstd"], w=[dkey])

        def silu2(pt, pk, dst, dkey, th, thk):
            op(ACT, lambda e: e.activation(out=th[:, 0:BW], in_=pt[:, 0:BW], func=AF.Tanh, scale=0.5), r=[pk], w=[thk])
            op(DVE, lambda e: e.scalar_tensor_tensor(out=dst, in0=th[:, 0:BW], scalar=1.0, in1=pt[:, 0:BW], op0=ALU.add, op1=ALU.mult), r=[thk, pk], w=[dkey])

        def others():
            with ExitStack() as ph:
                vp = sb("vp", [128, Ls + 16], F32, ph)
                sA = sb("sA", [128, Ls + 16], F32, ph); sB = sb("sB", [128, Ls + 16], F32, ph)
                dff = sb("dff", [128, Ls], BF16, ph)
                sgr = Ring("sgp", [sb("sgp%d" % i, [128, 512], BF16, ph) for i in range(2)])
                thr = Ring("thp", [sb("thp%d" % i, [128, 512], BF16, ph) for i in range(2)])
                op(POOL, lambda e: e.memset(vp[:, 0:8], 0.0), w=["vp"])
                op(POOL, lambda e: e.memset(vp[:, Ls + 8:Ls + 16], 0.0), w=["vp"])
                wt, wk = load_w(l, 0, 2)
                wg, wgk = load_w(l, 2, 2)
                for ci in range(2):
                    for n in range(NBk):
                        pt, pk = proj(wt, wk, ci, n, BW, OLO, OHI)
                        op(DVE, lambda e: e.tensor_copy(out=vp[:, 8 + n * BW:8 + (n + 1) * BW], in_=pt[:, 0:BW]), r=[pk], w=["vp"])
                        yield
                    W = Ls + 16
                    op(POOL, lambda e: e.tensor_tensor(out=sA[:, 1:W], in0=vp[:, 0:W - 1], in1=vp[:, 1:W], op=ALU.add), r=["vp"], w=["sA"])
                    if ci == 0:
                        op(POOL, lambda e: e.tensor_tensor(out=sB[64:128, 2:W - 1], in0=sA[64:128, 1:W - 2], in1=sA[64:128, 3:W], op=ALU.add), r=["sA"], w=["sB"])
                        halves = [(0, sA, "sA", 2), (1, sB, "sB", 4)]
                    else:
                        op(POOL, lambda e: e.tensor_tensor(out=sB[:, 2:W - 1], in0=sA[:, 1:W - 2], in1=sA[:, 3:W], op=ALU.add), r=["sA"], w=["sB"])
                        op(POOL, lambda e: e.tensor_tensor(out=sA[:, 4:W - 3], in0=sB[:, 2:W - 5], in1=sB[:, 6:W - 1], op=ALU.add), r=["sB"], w=["sA"])
                        op(POOL, lambda e: e.tensor_tensor(out=sB[64:128, 8:W - 7], in0=sA[64:128, 4:W - 11], in1=sA[64:128, 12:W - 3], op=ALU.add), r=["sA"], w=["sB"])
                        halves = [(0, sA, "sA", 8), (1, sB, "sB", 16)]
                    yield
                    for hf, st_, stk, w_ in halves:
                        rows = slice(hf * 64, (hf + 1) * 64)
                        op(POOL, lambda e: e.tensor_tensor(out=st_[rows, 8:16], in0=st_[rows, 8:16], in1=pcorr[rows, ci, 0:8], op=ALU.mult), r=[stk], w=[stk])
                        op(POOL, lambda e: e.tensor_tensor(out=st_[rows, Ls:Ls + 8], in0=st_[rows, Ls:Ls + 8], in1=pcorr[rows, ci, 8:16], op=ALU.mult), r=[stk], w=[stk])
                        yield
                        op(DVE, lambda e: e.scalar_tensor_tensor(out=dff[rows, :], in0=st_[rows, 8:Ls + 8], scalar=1.0 / w_, in1=vp[rows, 8:Ls + 8], op0=ALU.mult, op1=ALU.subtract),
                           r=[stk, "vp"], w=["dff"])
                        yield
                    for n in range(NBk):
                        gp_, gk_ = proj(wg, wgk, ci, n, BW, OLO, OHI)
                        sg, sgk, _ = sgr.next(); th, thk, _ = thr.next()
                        silu2(gp_, gk_, sg[:, 0:BW], sgk, th, thk)
                        yield
                        yp, yk = psr.next(OLO, OHI)
                        op(PE, lambda e: e.matmul(yp[:, 0:BW], lhsT=pwblk[:, l, ci, :], rhs=dff[:, n * BW:(n + 1) * BW], start=True, stop=True), r=["dff", "pwblk"], w=[yk])
                        yield
                        op(DVE, lambda e: e.scalar_tensor_tensor(out=cat[:, ci, n * BW:(n + 1) * BW], in0=yp[:, 0:BW], scalar=psT[:, l, ci:ci + 1], in1=sg[:, 0:BW], op0=ALU.mult, op1=ALU.mult),
                           r=[yk, sgk], w=["cat"])
            pb.barrier()
            yield
            with ExitStack() as phu:
                u_bf = sb("u_bf", [128, 2, Ls], BF16, phu); g0 = sb("g0", [128, 2, Ls], BF16, phu)
                with ExitStack() as ph:
                    pin = sb("pin", [128, Ls + 2], F32, ph)
                    coA = sb("coA", [128, Ls], F32, ph); coB = sb("coB", [128, Ls], F32, ph)
                    sgt = sb("sgt", [128, Ls], BF16, ph)
                    thr = Ring("thh", [sb("thh%d" % i, [128, 512], BF16, ph) for i in range(2)])
                    op(POOL, lambda e: e.memset(pin[:, 0:1], 0.0), w=["pin"])
                    op(POOL, lambda e: e.memset(pin[:, Ls + 1:Ls + 2], 0.0), w=["pin"])

                    def conv(wt, wk, mi, mm, co, cok):
                        for n in range(NBk):
                            pt, pk = proj(wt, wk, mi, n, BW, OLO, OHI)
                            op(DVE, lambda e: e.tensor_copy(out=pin[:, 1 + n * BW:1 + (n + 1) * BW], in_=pt[:, 0:BW]), r=[pk], w=["pin"])
                            yield
                        op(DVE, lambda e: e.tensor_scalar(out=co, in0=pin[:, 1:Ls + 1], scalar1=cwT[:, l, mm, 1:2], scalar2=cbT[:, l, mm:mm + 1], op0=ALU.mult, op1=ALU.add), r=["pin"], w=[cok])
                        yield
                        op(DVE, lambda e: e.scalar_tensor_tensor(out=co, in0=pin[:, 0:Ls], scalar=cwT[:, l, mm, 0:1], in1=co, op0=ALU.mult, op1=ALU.add), r=["pin", cok], w=[cok])
                        yield
                        op(DVE, lambda e: e.scalar_tensor_tensor(out=co, in0=pin[:, 2:Ls + 2], scalar=cwT[:, l, mm, 2:3], in1=co, op0=ALU.mult, op1=ALU.add), r=["pin", cok], w=[cok])
                        yield

                    w_x1, k_x1 = load_w(l, 6, 2)
                    w_v, k_v = load_w(l, 8, 2)
                    for ci in range(2):
                        yield from conv(w_x1, k_x1, ci, 2 + ci, coA, "coA")
                        yield from conv(w_v, k_v, ci, 4 + ci, coB, "coB")
                        op(DVE, lambda e: e.tensor_tensor(out=u_bf[:, ci, 0:Ls], in0=coA, in1=coB, op=ALU.mult), r=["coA", "coB"], w=["u_bf"])
                        yield
                    w_x0, k_x0 = load_w(l, 4, 2)
                    w_g, k_g = load_w(l, 10, 2)
                    for ci in range(2):
                        yield from conv(w_x0, k_x0, ci, ci, coA, "coA")
                        for n in range(NBk):
                            pt, pk = proj(w_g, k_g, ci, n, BW, OLO, OHI)
                            th, thk, _ = thr.next()
                            silu2(pt, pk, sgt[:, n * BW:(n + 1) * BW], "sgt", th, thk)
                            yield
                        op(DVE, lambda e: e.scalar_tensor_tensor(out=g0[:, ci, 0:Ls], in0=coA, scalar=0.5, in1=sgt, op0=ALU.mult, op1=ALU.mult), r=["coA", "sgt"], w=["g0"])
                        yield
                dump("ubf_%d_%d_%d" % (l, b, int(is_ctx)), u_bf[:, :, 0:Ls], ["u_bf"])
                dump("g0_%d_%d_%d" % (l, b, int(is_ctx)), g0[:, :, 0:Ls], ["g0"])
                pb.barrier()
                yield
                if VERBOSE:
                    print("H2 start step", ilc[0], "Ls", Ls)
                TC = NT; NP_ = NT; NJ = 2 * NT
                nmm = "c" if is_ctx else "x"
                with ExitStack() as ph:
                    uT = sb("uT", [128, TC, 256], BF16, ph)
                    Y = sb("Y", [128, NJ, 256], BF16, ph)
                    wfr = Ring("wf", [sb("wf%d" % i, [128, TC, 128], BF16, ph) for i in range(2)])
                    khr = Ring("khr", [sb("khr%d" % i, [128, 2, 256], F32, ph) for i in range(2)])
                    tt = [sb("ht%d" % i, [128, 256], F32, ph) for i in range(4)]
                    accs = [sb("hacc%d" % i, [128, 512], F32, ph) for i in range(2)]
                    if is_ctx:
                        wic = sb("wic", [128, NJ, Ls], BF16, ph)
                        dma(SP, wic, cd["Wi_c"], (), ["wic"], "wic")
                    else:
                        wir = Ring("wir", [sb("wir%d" % i, [128, 4, 512], BF16, ph) for i in range(2)])
                    for t0 in range(0, TC, 4):
                        nt4 = min(4, TC - t0)
                        pt, pk = psr.next(OLO, OHI)
                        ptb = pt.bitcast(BF16)
                        for a_ in range(nt4):
                            for ci in range(2):
                                op(PE, lambda e: e.transpose(ptb[:, (a_ * 2 + ci) * 128:(a_ * 2 + ci + 1) * 128], u_bf[:, ci, (t0 + a_) * 128:(t0 + a_ + 1) * 128], ident_bf),
                                   r=["u_bf"], w=[pk], sig=(a_ == nt4 - 1 and ci == 1))
                        yield
                        op(DVE, lambda e: e.tensor_copy(out=uT[:, t0:t0 + nt4, :], in_=ptb[:, 0:nt4 * 256].rearrange("p (a c) -> p a c", c=256)), r=[pk], w=["uT"])
                        yield
                    if VERBOSE:
                        print("H2 fwd start step", ilc[0])
                    for jp in range(NP_):
                        pss = []
                        for comp in range(2):
                            wt, wk, ws = wfr.next()
                            dma(SP, wt, cd["Wf_" + nmm][comp * NP_ + jp], (), [wk], ws)
                            pt, pk = psum[OLO + comp], ("ps", OLO + comp)
                            for tc in range(TC):
                                op(PE, lambda e: e.matmul(pt[:, 0:256], lhsT=wt[:, tc, :], rhs=uT[:, tc, :], start=(tc == 0), stop=(tc == TC - 1)), r=[wk, "uT"], w=[pk], sig=(tc == TC - 1))
                            pss.append((pt, pk))
                            yield
                        kt_, kk, ks = khr.next()
                        dma(SP, kt_, (khc_d if is_ctx else kh_d[l])[jp], [("kh", l, nmm)], [kk], ks)
                        kre, kim = kt_[:, 0, :], kt_[:, 1, :]
                        (pr, prk), (pi_, pik) = pss
                        op(DVE, lambda e: e.tensor_tensor(out=tt[0], in0=pr[:, 0:256], in1=kre, op=ALU.mult), r=[prk, kk], w=["ht0"])
                        op(DVE, lambda e: e.tensor_tensor(out=tt[1], in0=pi_[:, 0:256], in1=kim, op=ALU.mult), r=[pik, kk], w=["ht1"])
                        op(DVE, lambda e: e.tensor_tensor(out=tt[2], in0=pr[:, 0:256], in1=kim, op=ALU.mult), r=[prk, kk], w=["ht2"])
                        op(DVE, lambda e: e.tensor_tensor(out=tt[3], in0=pi_[:, 0:256], in1=kre, op=ALU.mult), r=[pik, kk], w=["ht3"])
                        yield
                        op(DVE, lambda e: e.tensor_tensor(out=Y[:, jp, :], in0=tt[0], in1=tt[1], op=ALU.subtract), r=["ht0", "ht1"], w=["Y"])
                        op(DVE, lambda e: e.tensor_tensor(out=Y[:, NP_ + jp, :], in0=tt[2], in1=tt[3], op=ALU.add), r=["ht2", "ht3"], w=["Y"])
                        if jp == 0:
                            op(DVE, lambda e: e.tensor_copy(out=Y[0:1, 0, :], in_=tt[0][0:1, :]), r=["ht0"], w=["Y"])
                            op(DVE, lambda e: e.tensor_copy(out=Y[0:1, NP_, :], in_=tt[1][0:1, :]), r=["ht1"], w=["Y"])
                        yield
                    if VERBOSE:
                        print("H2 inv start step", ilc[0])
                    for n in range(NBk):
                        ops_ = [(psum[OLO], ("ps", OLO)), (psum[OLO + 1], ("ps", OLO + 1))]
                        if is_ctx:
                            for ci in range(2):
                                for j in range(NJ):
                                    op(PE, lambda e: e.matmul(ops_[ci][0][:, 0:BW], lhsT=Y[:, j, ci * 128:(ci + 1) * 128], rhs=wic[:, j, :], start=(j == 0), stop=(j == NJ - 1)),
                                       r=["Y", "wic"], w=[ops_[ci][1]], sig=(j == NJ - 1))
                                yield
                        else:
                            for g in range(NJ // 4):
                                wt, wk, ws = wir.next()
                                dma(SP, wt, cd["Wi_x"][n, g], (), [wk], ws)
                                for ci in range(2):
                                    for jj in range(4):
                                        j = g * 4 + jj
                                        op(PE, lambda e: e.matmul(ops_[ci][0][:, 0:BW], lhsT=Y[:, j, ci * 128:(ci + 1) * 128], rhs=wt[:, jj, :], start=(jj == 0), stop=(jj == 3)),
                                           r=["Y", wk], w=[ops_[ci][1]], sig=(jj == 3))
                                for ci in range(2):
                                    if g == 0:
                                        op(DVE, lambda e: e.tensor_copy(out=accs[ci][:, 0:BW], in_=ops_[ci][0][:, 0:BW]), r=[ops_[ci][1]], w=[("hacc", ci)])
                                    else:
                                        op(DVE, lambda e: e.tensor_tensor(out=accs[ci][:, 0:BW], in0=accs[ci][:, 0:BW], in1=ops_[ci][0][:, 0:BW], op=ALU.add), r=[ops_[ci][1], ("hacc", ci)], w=[("hacc", ci)])
                                yield
                        for ci in range(2):
                            if is_ctx:
                                op(DVE, lambda e: e.scalar_tensor_tensor(out=accs[ci][:, 0:BW], in0=u_bf[:, ci, n * BW:(n + 1) * BW], scalar=hbT[:, l, ci:ci + 1], in1=ops_[ci][0][:, 0:BW], op0=ALU.mult, op1=ALU.add),
                                   r=["u_bf", ops_[ci][1]], w=[("hacc", ci)])
                            else:
                                op(DVE, lambda e: e.scalar_tensor_tensor(out=accs[ci][:, 0:BW], in0=u_bf[:, ci, n * BW:(n + 1) * BW], scalar=hbT[:, l, ci:ci + 1], in1=accs[ci][:, 0:BW], op0=ALU.mult, op1=ALU.add),
                                   r=["u_bf", ("hacc", ci)], w=[("hacc", ci)])
                            op(DVE, lambda e: e.tensor_tensor(out=cat[:, 2 + ci, n * BW:(n + 1) * BW], in0=accs[ci][:, 0:BW], in1=g0[:, ci, n * BW:(n + 1) * BW], op=ALU.mult), r=[("hacc", ci), "g0"], w=["cat"])
                            yield

        with ExitStack() as ph:
            tmp = (sb("nr_sq", [128, 512], BF16, ph), sb("nr_qg", [128, 512], BF16, ph), sb("nr_rstd", [128, 512], F32, ph),
                   sb("nr_t1", [128, 512], F32, ph), sb("nr_t2", [128, 512], F32, ph))
            awr = Ring("aw", [sb("awslot%d" % i, [128, 2, KC, 128], BF16, ph) for i in range(2)])

            def load_wa(m0):
                wt, wk, ws = awr.next()
                dma(SP, wt, winbf_d[l, m0:m0 + 2].rearrange("m p k c -> p m k c"), (), [wk], ws)
                return wt, wk

            if is_ctx:
                kT = kTc; vE = vcx
            else:
                kT = sb("kT", [128, 2, NK], BF16, ph); vE = sb("vE", [128, NKC, 128], BF16, ph)
            wt, wk = load_wa(22)
            for h in range(2):
                for n in range(NBk):
                    pt, pk = proj(wt, wk, h, n, BW)
                    normrope(pt, pk, gk[:, l:l + 1], kT[:, h, n * BW:(n + 1) * BW], "kT", n, not is_ctx, tmp, 0, 8)
            wt, wk = load_wa(16)
            for i0 in range(0, NT, 4):
                pt, pk = psr.next()
                ni = min(4, NT - i0)
                for ii in range(ni):
                    i = i0 + ii
                    for k in range(KC):
                        op(PE, lambda e: e.matmul(pt[:, ii * 128:(ii + 1) * 128], lhsT=hxT[:, k, i * 128:(i + 1) * 128], rhs=wt[:, 1, k, :], start=(k == 0), stop=(k == KC - 1)),
                           r=[wk, "hxT"], w=[pk], sig=(ii == ni - 1 and k == KC - 1))
                op(ACT, lambda e: e.copy(out=vE[:, i0:i0 + ni, :], in_=pt[:, 0:ni * 128].rearrange("p (a c) -> p a c", c=128)), r=[pk], w=["vE"])
            if not is_ctx:
                op(DVE, lambda e: e.tensor_copy(out=kT[:, :, Ls:Ls + LC], in_=kTc), r=["kT"], w=["kT"])
                op(DVE, lambda e: e.tensor_copy(out=vE[:, NT:NT + 2, :], in_=vcx), r=["vE"], w=["vE"])
            dump("kTc_%d_%d_%d" % (l, b, int(is_ctx)), kTc, ["kT"])
            dump("vcx_%d_%d_%d" % (l, b, int(is_ctx)), vcx, ["vE"])
            if not kv_only:
                qT = sb("qT", [128, Ls], BF16, ph)
                pTr = Ring("pT", [sb("pT%d" % i, [128, 2, 512], BF16, ph) for i in range(3)])
                sga = sb("sga", [128, Ls], BF16, ph)
                tha = sb("tha", [128, 512], BF16, ph)
                osb = sb("osb", [128, 512], F32, ph); dsb = sb("dsb", [128, 512], F32, ph)
                rdn = sb("rdn", [128, 512], F32, ph); ot = sb("ot", [128, 512], F32, ph)
                gen = others()
                ilc = [0]

                def gstep():
                    if ilc[0] < INTERLEAVE:
                        ilc[0] += 1
                        next(gen, None)
                scnt = [0]
                for m in range(4):
                    h = m // 2
                    if m % 2 == 0:
                        wq, wqk = load_wa(12 + m)
                        wg, wgk = load_wa(18 + m)
                    for n in range(NBk):
                        pt, pk = proj(wq, wqk, m % 2, n, BW, ALO, AHI)
                        normrope(pt, pk, gq[:, l:l + 1], qT[:, n * BW:(n + 1) * BW], "qT", n, not is_ctx, tmp, ALO, AHI)
                        gp_, gk_ = proj(wg, wgk, m % 2, n, BW, ALO, AHI)
                        silu2(gp_, gk_, sga[:, n * BW:(n + 1) * BW], "sga", tha, "tha")
                        gstep()

                    def emit_SE(n, kc):
                        pT, pTk, _ = pTr.next()
                        pr_ = scnt[0] % 2
                        scnt[0] += 1
                        b0 = ALO + 2 * pr_
                        for hf in range(2):
                            rows = slice(hf * 64, (hf + 1) * 64)
                            op(PE, lambda e: e.matmul(psum[b0 + hf][:, 0:BW], lhsT=kT[rows, h, kc * 128:(kc + 1) * 128], rhs=qT[rows, n * BW:(n + 1) * BW], start=True, stop=True),
                               r=["kT", "qT"], w=[("ps", b0 + hf)], sig=(hf == 1))
                        op(ACT, lambda e: e.activation(out=pT[:, :, 0:BW], in_=psum_all[:, b0 * 512:(b0 + 2) * 512].rearrange("p (h c) -> p h c", h=2)[:, :, 0:BW], func=AF.Exp, scale=0.125),
                           r=[("ps", b0), ("ps", b0 + 1)], w=[pTk])
                        return pT, pTk

                    def emit_V(n, kc, pT, pTk):
                        o_ps = psum[0]; d_ps = psum[1]
                        okey = ("ps", 0); dkey = ("ps", 1)
                        for hf in range(2):
                            rows = slice(hf * 64, (hf + 1) * 64)
                            op(PE, lambda e: e.matmul(o_ps[rows, 0:BW], lhsT=vE[:, kc, h * 64:(h + 1) * 64], rhs=pT[:, hf, 0:BW], start=(kc == 0), stop=(kc == NKC - 1)),
                               r=["vE", pTk], w=[okey], sig=False)
                        for hf in range(2):
                            rows = slice(hf * 64, (hf + 1) * 64)
                            op(PE, lambda e: e.matmul(d_ps[rows, 0:BW], lhsT=ones_bf, rhs=pT[:, hf, 0:BW], start=(kc == 0), stop=(kc == NKC - 1)),
                               r=["ones_bf", pTk], w=[dkey], sig=(hf == 1))
                        if kc == NKC - 1:
                            op(DVE, lambda e: e.tensor_copy(out=dsb[:, 0:BW], in_=d_ps[:, 0:BW]), r=[dkey], w=["dsb"])
                            op(DVE, lambda e: e.tensor_copy(out=osb[:, 0:BW], in_=o_ps[:, 0:BW]), r=[okey], w=["osb"])
                            op(DVE, lambda e: e.reciprocal(out=rdn[:, 0:BW], in_=dsb[:, 0:BW]), r=["dsb"], w=["rdn"])
                            op(DVE, lambda e: e.tensor_tensor(out=ot[:, 0:BW], in0=osb[:, 0:BW], in1=rdn[:, 0:BW], op=ALU.mult), r=["osb", "rdn"], w=["ot"])
                            op(DVE, lambda e: e.scalar_tensor_tensor(out=cat[:, 4 + m, n * BW:(n + 1) * BW], in0=ot[:, 0:BW], scalar=0.5, in1=sga[:, n * BW:(n + 1) * BW], op0=ALU.mult, op1=ALU.mult), r=["ot", "sga"], w=["cat"])

                    iters = [(n, kc) for n in range(NBk) for kc in range(NKC)]
                    pend = []
                    for t in range(len(iters) + 1):
                        if t < len(iters):
                            n_, kc_ = iters[t]
                            pend.append((n_, kc_) + emit_SE(n_, kc_))
                        if t >= 1:
                            emit_V(*pend[t - 1])
                            gstep()
                for _ in gen:
                    pass
        dump("catH_%d_%d_%d" % (l, b, int(is_ctx)), cat[:, :, 0:Ls], ["cat"])
        pb.barrier()
        if kv_only:
            return

        with ExitStack() as ph:
            gb = sb("gb", [128, KC, 128], F32, ph)
            lgb = sb("lgb", [128, 2, D], F32, ph)
            wout_sb = sb("wout_sb", [128, KC, D], BF16, ph)
            Gp = sb("Gp", [128, D], F32, ph)
            GO = min(2, NT); NGO = NT // GO
            zg = [sb("O_z%d" % i, [128, GO, D], F32, ph) for i in range(4)]
            xo = [sb("O_x%d" % i, [128, GO, D], F32, ph) for i in range(2)]
            sts = [sb("O_st%d" % i, [128, GO, 12], F32, ph) for i in range(3)]
            mvs = [sb("O_mv%d" % i, [128, GO, 2], F32, ph) for i in range(3)]
            rss = [sb("O_rs%d" % i, [128, GO, 3], F32, ph) for i in range(3)]
            dma(SP, wout_sb, woutbf_d[l], (), ["wout"], "wout")
            dma(SP, lgb[:, 0, :], lng_d[l:l + 1, :].partition_broadcast(128), (), ["lgb"], "lgb")
            dma(SP, lgb[:, 1, :], lnb_d[l:l + 1, :].partition_broadcast(128), (), ["lgb"], "lgb")
            op(DVE, lambda e: e.tensor_scalar(out=gb, in0=modT[:, l, 16:24, col:col + 1].to_broadcast([128, KC, 128]), scalar1=1.0 / ALPHA, scalar2=None, op0=ALU.mult), w=["gb"])
            for hh in range(2):
                pt, pk = psum[hh], ("ps", hh)
                for kk_ in range(4):
                    k = hh * 4 + kk_
                    op(PE, lambda e: e.matmul(pt[:, kk_ * 128:(kk_ + 1) * 128], lhsT=gb[:, k, :], rhs=ident_f, start=True, stop=True), r=["gb"], w=[pk])
                op(ACT, lambda e: e.copy(out=Gp[:, hh * 512:(hh + 1) * 512], in_=pt), r=[pk], w=["Gp"])

            def o1_(g):
                xt = xo[g % 2]; z = zg[g % 4]; st = sts[g % 3]; mv = mvs[g % 3]
                dma(SP, xt, src_d[g * GO * 128:(g + 1) * GO * 128, :].rearrange("(a p) d -> p a d", p=128), [(dkey_src, g * GO + a) for a in range(GO)], [("O_x", g % 2)], "O_x%d" % (g % 2))
                b0 = (g % 2) * 4
                for a in range(GO):
                    i = g * GO + a
                    for hh in range(2):
                        bk = b0 + a * 2 + hh
                        for k in range(KC):
                            op(PE, lambda e: e.matmul(psum[bk], lhsT=cat[:, k, i * 128:(i + 1) * 128], rhs=wout_sb[:, k, hh * 512:(hh + 1) * 512], start=(k == 0), stop=(k == KC - 1)),
                               r=["cat", "wout"], w=[("ps", bk)], sig=(k == KC - 1))
                    bb = b0 + a * 2
                    op(DVE, lambda e: e.tensor_tensor(out=z[:, a, :], in0=psum_all[:, bb * 512:(bb + 2) * 512], in1=Gp, op=ALU.mult), r=[("ps", bb), ("ps", bb + 1), "Gp"], w=[("O_z", g % 4)])
                op(DVE, lambda e: e.tensor_tensor(out=z, in0=z, in1=xt, op=ALU.add), r=[("O_z", g % 4), ("O_x", g % 2)], w=[("O_z", g % 4)])
                for a in range(GO):
                    for hh in range(2):
                        op(DVE, lambda e: e.bn_stats(out=st[:, a, hh * 6:(hh + 1) * 6], in_=z[:, a, hh * 512:(hh + 1) * 512]), r=[("O_z", g % 4)], w=[("O_st", g % 3)])
                    op(DVE, lambda e: e.bn_aggr(out=mv[:, a, :], in_=st[:, a, :]), r=[("O_st", g % 3)], w=[("O_mv", g % 3)])

            def o2_(g):
                mv = mvs[g % 3]; rs = rss[g % 3]
                op(ACT, lambda e: e.activation(out=rs[:, :, 0], in_=mv[:, :, 1], func=AF.Sqrt, bias=epsc[:, 1:2], scale=1.0), r=[("O_mv", g % 3)], w=[("O_rs0", g % 3)])

            def o3_(g):
                mv = mvs[g % 3]; rs = rss[g % 3]
                op(DVE, lambda e: e.reciprocal(out=rs[:, :, 1], in_=rs[:, :, 0]), r=[("O_rs0", g % 3)], w=[("O_rs1", g % 3)])
                op(DVE, lambda e: e.scalar_tensor_tensor(out=rs[:, :, 2], in0=mv[:, :, 0], scalar=-1.0, in1=rs[:, :, 1], op0=ALU.mult, op1=ALU.mult), r=[("O_mv", g % 3), ("O_rs1", g % 3)], w=[("O_rs2", g % 3)])

            def o4_(g):
                z = zg[g % 4]; rs = rss[g % 3]
                for a in range(GO):
                    op(ACT, lambda e: e.activation(out=z[:, a, :], in_=z[:, a, :], func=AF.Identity, bias=rs[:, a, 2:3], scale=rs[:, a, 1:2]), r=[("O_z", g % 4), ("O_rs1", g % 3), ("O_rs2", g % 3)], w=[("O_z", g % 4)])

            def o5_(g):
                z = zg[g % 4]
                for a in range(GO):
                    op(POOL, lambda e: e.tensor_tensor(out=z[:, a, :], in0=z[:, a, :], in1=lgb[:, 0, :], op=ALU.mult), r=[("O_z", g % 4), "lgb"], w=[("O_z", g % 4)])
                    op(POOL, lambda e: e.tensor_tensor(out=z[:, a, :], in0=z[:, a, :], in1=lgb[:, 1, :], op=ALU.add), r=[("O_z", g % 4), "lgb"], w=[("O_z", g % 4)])
                dma(POOL, dst_d[g * GO * 128:(g + 1) * GO * 128, :].rearrange("(a p) d -> p a d", p=128), z, [("O_z", g % 4)], [(dkey_dst, g * GO + a) for a in range(GO)], "O_zs%d" % (g % 4))

            pipeline(NGO, [o1_, o2_, o3_, o4_, o5_])
        pb.barrier()

    for b in range(NB):
        seq_block(0, b, LC, ctx_d[b], c1_d, True, False, ("ctxin", b), "c1")
        seq_block(0, b, L, x_d[b], x1_d, False, False, ("xin", b), "x1")
        if dbg and b == 0 and "x1" in dbg:
            pass
        seq_block(1, b, LC, c1_d, None, True, True, "c1", None)
        seq_block(1, b, L, x1_d, out_d[b], False, False, "x1", ("out", b))
    pb.full_barrier()
    es.close()
    return nc, consts


_CACHE = {}


def kernel(**inputs):
    NB = inputs["x"].shape[0] // N_CORES
    if NB not in _CACHE:
        _CACHE[NB] = build(NB)
    nc, consts = _CACHE[NB]
    in_maps = []
    for ci in range(N_CORES):
        sl = slice(ci * NB, (ci + 1) * NB)
        m = {}
        for k, v in inputs.items():
            a = np.asarray(v)
            if k in ("x", "c", "ctx"):
                a = a[sl]
            elif k == "c_ctx":
                a = a.reshape(1, D)
            m[k] = np.ascontiguousarray(a, dtype=np.float32)
        for k, v in consts.items():
            m["k_" + k] = v
        in_maps.append(m)
    res = run_bass_kernel_spmd(nc, in_maps, core_ids=list(range(N_CORES)))
    out = np.concatenate([np.asarray(r["out"]) for r in res.results], axis=0)
    return out.astype(np.float32)
```

```python
import math
from contextlib import ExitStack
import numpy as np
import ml_dtypes
import concourse.bass as bass
import concourse.mybir as mybir
from concourse.bass_utils import run_bass_kernel_spmd

F32 = mybir.dt.float32
BF16 = mybir.dt.bfloat16
AF = mybir.ActivationFunctionType
ALU = mybir.AluOpType
NPBF = ml_dtypes.bfloat16

D = 1024
KC = 8
L = 2048
LC = 256
DEPTH = 2
INW = 2816
NCH = 24
ALPHA = (2.0 * DEPTH) ** 0.25
LN_EPS = 1e-6
QK_EPS = 1e-6
MAGIC = 12582912.0
N_CORES = 8
INTERLEAVE = 100000
VERBOSE = False


def _dft_mats(Lh):
    N = 2 * Lh
    t = np.arange(Lh, dtype=np.float64)
    f = np.arange(Lh, dtype=np.float64)
    ang = 2.0 * np.pi * np.outer(t, f) / N
    Wf = np.zeros((Lh, N), np.float64)
    Wf[:, :Lh] = np.cos(ang)
    Wf[:, Lh:] = -np.sin(ang)
    Wf[:, Lh] = np.cos(np.pi * t)
    Wi = np.zeros((N, Lh), np.float64)
    cf = np.full(Lh, 2.0); cf[0] = 1.0
    Wi[:Lh, :] = (cf[:, None] / N) * np.cos(ang.T)
    Wi[Lh:, :] = -(2.0 / N) * np.sin(ang.T)
    Wi[Lh, :] = np.cos(np.pi * t) / N
    return Wf, Wi


def make_consts():
    c = {}
    c["ident_bf"] = np.eye(128, dtype=np.float32).astype(NPBF)
    c["ident_f"] = np.eye(128, dtype=np.float32)
    bo = (np.arange(128)[:, None] // 64 == np.arange(128)[None, :] // 64).astype(np.float32)
    c["bones"] = bo.astype(NPBF)
    c["ones_f"] = np.ones((128, 128), np.float32)
    P = np.zeros((128, 128), np.float32)
    for m in range(128):
        r = (m % 64) % 32
        if r < 16:
            P[m + 16, m] = -1.0
        else:
            P[m - 16, m] = 1.0
    c["rotP"] = P.astype(NPBF)
    t = np.arange(L)
    row = (t // 64).astype(np.float32); col = (t % 64).astype(np.float32)
    inv = (10000.0 ** (-np.arange(0, 32, 2, dtype=np.float32) / 32)).astype(np.float32)
    ang = np.concatenate([row[:, None] * inv, col[:, None] * inv], -1).astype(np.float32)
    idx = np.array([((p % 64) // 32) * 16 + (p % 16) for p in range(128)])
    c["ropeC"] = np.cos(ang)[:, idx].T.astype(NPBF).copy()
    c["ropeS"] = np.sin(ang)[:, idx].T.astype(NPBF).copy()
    wins = (2, 4, 8, 16)
    pcorr = np.ones((128, 2, 16), np.float32)
    for ci in range(2):
        for p in range(128):
            w = wins[2 * ci + p // 64]
            for i in range(8):
                if i < w // 2:
                    pcorr[p, ci, i] = w / (i + w // 2)
                pcorr[p, ci, 8 + i] = w / min(w, 8 - i + w // 2)
    c["pcorr"] = pcorr
    for nm, Lh in (("x", L), ("c", LC)):
        tt = np.linspace(0.0, 1.0, Lh, dtype=np.float32)[:, None]
        fr = np.linspace(1e-4, 15, 16, dtype=np.float32)
        wpos = (2.0 * math.pi * np.arange(Lh, dtype=np.float32)[:, None] / Lh).astype(np.float32)
        z = np.concatenate([tt, np.cos(fr * wpos), -np.sin(fr * wpos)], -1).astype(np.float32)
        c["zT_" + nm] = z.T.copy()
        max_decay = math.log(1e-2) / 0.3
        min_decay = math.log(1e-2) / 1.5
        deltas = np.linspace(min_decay, max_decay, 256, dtype=np.float32)
        dec = np.exp(-tt * np.abs(deltas)).astype(np.float32)
        dec2 = np.concatenate([dec, dec], 1)
        dec2[0, 256:] = 0.0
        c["dec_" + nm] = dec2.reshape(Lh // 128, 128, 512).transpose(1, 0, 2).copy()
        Wf, Wi = _dft_mats(Lh)
        NJ = 2 * Lh // 128; TC = Lh // 128
        c["Wf_" + nm] = Wf.reshape(TC, 128, NJ, 128).transpose(2, 1, 0, 3).astype(np.float32).astype(NPBF).copy()
        if nm == "x":
            c["Wi_x"] = Wi.reshape(NJ // 4, 4, 128, 4, 512).transpose(3, 0, 2, 1, 4).astype(np.float32).astype(NPBF).copy()
        else:
            c["Wi_c"] = Wi.reshape(NJ, 128, Lh).transpose(1, 0, 2).astype(np.float32).astype(NPBF).copy()
    return c


class PB:
    LIM = 30000

    def __init__(self, nc):
        self.nc = nc
        self.E = {"pe": nc.tensor, "act": nc.scalar, "dve": nc.vector, "pool": nc.gpsimd, "sp": nc.sync}
        self.sems = {}
        self.owner = {}
        self.seen = {e: {} for e in self.E}
        self.lw = {}
        self.rd = {}
        self.nsem = 0
        self.nops = 0

    def _sem(self, name):
        s = self.sems.get(name)
        if s is None or s[1] >= self.LIM:
            h = self.nc.alloc_semaphore("s%d" % self.nsem)
            self.nsem += 1
            s = [h, 0]
            self.sems[name] = s
            if name.startswith("E_"):
                self.owner[id(h)] = name[2:]
        return s

    def _wait(self, e, tok):
        h, cnt = tok
        if self.owner.get(id(h)) == e and e == "pe":
            return
        sn = self.seen[e]
        if sn.get(id(h), 0) >= cnt:
            return
        self.E[e].wait_ge(h, cnt)
        sn[id(h)] = cnt

    def op(self, e, fn, r=(), w=(), sig=True, dma=None):
        deps = []
        for k in r:
            t = self.lw.get(k)
            if t is not None:
                deps.append(t)
        for k in w:
            t = self.lw.get(k)
            if t is not None:
                deps.append(t)
            deps.extend(self.rd.get(k, ()))
        for t in deps:
            self._wait(e, t)
        ins = fn(self.E[e])
        self.nops += 1
        if dma is not None:
            s = self._sem(dma)
            s[1] += 16
            ins.then_inc(s[0], 16)
            tok = (s[0], s[1])
        else:
            s = self._sem("E_" + e)
            if sig:
                s[1] += 1
                ins.then_inc(s[0], 1)
                tok = (s[0], s[1])
            else:
                tok = (s[0], s[1] + 1)
        for k in w:
            self.lw[k] = tok
            self.rd[k] = []
        for k in r:
            self.rd.setdefault(k, []).append(tok)
        return tok

    def barrier(self):
        toks = [(s[0], s[1]) for n, s in self.sems.items()
                if s[1] > 0 and (n.startswith("E_") or n.startswith("O_zs") or n.startswith("dbg"))]
        for e in self.E:
            for t in toks:
                self._wait(e, t)

    def full_barrier(self):
        alltoks = [(s[0], s[1]) for s in self.sems.values() if s[1] > 0]
        for e in self.E:
            for t in alltoks:
                self._wait(e, t)
        self.lw.clear()
        self.rd.clear()


def pipeline(n, stages):
    S = len(stages)
    for t in range(n + S - 1):
        for si in range(S - 1, -1, -1):
            i = t - si
            if 0 <= i < n:
                stages[si](i)


class Ring:
    def __init__(self, name, aps):
        self.name = name
        self.aps = aps
        self.i = 0

    def next(self):
        i = self.i % len(self.aps)
        self.i += 1
        return self.aps[i], (self.name, i), "%s%d" % (self.name, i)


def build(NB, dbg=None):
    nc = bass.Bass("TRN2", target_bir_lowering=False)
    pb = PB(nc)
    consts = make_consts()

    def din(name, shape, dt=F32):
        return nc.dram_tensor(name, list(shape), dt, kind="ExternalInput").ap()

    x_d = din("x", [NB, L, D]); c_d = din("c", [NB, D]); ctx_d = din("ctx", [NB, LC, D]); cctx_d = din("c_ctx", [1, D])
    wada_d = din("w_ada", [2, D, 3 * D]); bada_d = din("b_ada", [2, 3 * D]); win_d = din("w_in", [2, D, INW])
    poolw_d = din("pool_w", [2, 4, 64, 64]); pscale_d = din("pool_scale", [2, 256])
    cw_d = din("hy_conv_w", [2, 3, 768]); cb_d = din("hy_conv_b", [2, 768])
    hw1_d = din("hy_w1", [2, 33, 64]); hb1_d = din("hy_b1", [2, 64]); hfq_d = din("hy_freq", [2, 2, 64])
    hw2_d = din("hy_w2", [2, 64, 64]); hb2_d = din("hy_b2", [2, 64]); hw3_d = din("hy_w3", [2, 64, 512])
    hbias_d = din("hy_bias", [2, 256]); qn_d = din("q_norm", [2, 64]); kn_d = din("k_norm", [2, 64])
    wout_d = din("w_out", [2, D, D]); lng_d = din("ln_g", [2, D]); lnb_d = din("ln_b", [2, D])
    cd = {}
    for k, v in consts.items():
        cd[k] = din("k_" + k, v.shape, BF16 if v.dtype == NPBF else F32)
    out_d = nc.dram_tensor("out", [NB, L, D], F32, kind="ExternalOutput").ap()
    winbf_d = nc.dram_tensor("winbf", [2, NCH, 128, KC, 128], BF16).ap()
    woutbf_d = nc.dram_tensor("woutbf", [2, 128, KC, D], BF16).ap()
    x1_d = nc.dram_tensor("x1s", [L, D], F32, **({"kind": "ExternalOutput"} if dbg else {})).ap()
    c1_d = nc.dram_tensor("c1s", [LC, D], F32, **({"kind": "ExternalOutput"} if dbg else {})).ap()
    kh_d = nc.dram_tensor("khs", [2, 16, 128, 2, 256], F32).ap()
    khc_d = nc.dram_tensor("khcs", [2, 128, 2, 256], F32).ap()

    es = ExitStack()

    uid = [0]

    def sb(name, shape, dt, stack=es):
        uid[0] += 1
        return stack.enter_context(nc.sbuf_tensor("%s_%d" % (name, uid[0]), list(shape), dt)).ap()

    NS = NB + 1
    ropeC = sb("ropeC", [128, L], BF16); ropeS = sb("ropeS", [128, L], BF16)
    ident_bf = sb("ident_bf", [128, 128], BF16); ident_f = sb("ident_f", [128, 128], F32)
    bones = sb("bones", [128, 128], BF16); rotP = sb("rotP", [128, 128], BF16)
    ones_f = sb("ones_f", [128, 128], F32); ones_bf = sb("ones_bf", [128, 64], BF16)
    pcorr = sb("pcorr", [128, 2, 16], F32)
    modT = sb("modT", [128, 2, 24, NS], F32)
    cwT = sb("cwT", [128, 2, 6, 3], F32); cbT = sb("cbT", [128, 2, 6], F32)
    hbT = sb("hbT", [128, 2, 2], F32); psT = sb("psT", [128, 2, 2], F32)
    gq = sb("gq", [128, 2], F32); gk = sb("gk", [128, 2], F32)
    pwblk = sb("pwblk", [128, 2, 2, 128], BF16)
    epsc = sb("epsc", [128, 4], F32)
    psum_all = es.enter_context(nc.psum_tensor("psall", [128, 4096], F32)).ap()
    psum = [psum_all[:, i * 512:(i + 1) * 512] for i in range(8)]

    class PS:
        def __init__(self):
            self.i = 0

        def next(self, lo=0, hi=8):
            n = hi - lo
            i = lo + (self.i % n)
            self.i += 1
            return psum[i], ("ps", i)

    psr = PS()
    SP, PE, ACT, DVE, POOL = "sp", "pe", "act", "dve", "pool"
    op = pb.op

    def dma(q, out, in_, r, w, sem, slow=False):
        return op(q, lambda e: e.dma_start(out=out, in_=in_, allow_slow_non_contiguous=slow) if slow else e.dma_start(out=out, in_=in_),
                  r=r, w=w, dma=sem)

    dumps = {}

    def dump(name, ap, keys):
        if not dbg or name not in dbg or name in dumps:
            return
        t = nc.dram_tensor("dbg_" + name, list(ap.shape), ap.dtype, kind="ExternalOutput").ap()
        dumps[name] = t
        dma(SP, t, ap, keys, [("dbg", name)], "dbg_" + name)

    cl = [(ident_bf, cd["ident_bf"]), (ident_f, cd["ident_f"]), (bones, cd["bones"]), (rotP, cd["rotP"]),
          (ones_f, cd["ones_f"]), (ropeC, cd["ropeC"]), (ropeS, cd["ropeS"]), (pcorr, cd["pcorr"])]
    for dst, src in cl:
        dma(SP, dst, src, (), (), "const")
    for l in range(2):
        for tp in range(3):
            dma(SP, cwT[:, l, :, tp], cw_d[l, tp:tp + 1, :].rearrange("o (m p) -> p (o m)", p=128), (), (), "const", slow=True)
        dma(SP, cbT[:, l, :], cb_d[l:l + 1, :].rearrange("o (m p) -> p (o m)", p=128), (), (), "const", slow=True)
        dma(SP, hbT[:, l, :], hbias_d[l:l + 1, :].rearrange("o (m p) -> p (o m)", p=128), (), (), "const", slow=True)
        dma(SP, psT[:, l, :], pscale_d[l:l + 1, :].rearrange("o (m p) -> p (o m)", p=128), (), (), "const", slow=True)
        for hf in range(2):
            dma(SP, gq[hf * 64:(hf + 1) * 64, l:l + 1], qn_d[l:l + 1, :].rearrange("o d -> d o"), (), (), "const", slow=True)
            dma(SP, gk[hf * 64:(hf + 1) * 64, l:l + 1], kn_d[l:l + 1, :].rearrange("o d -> d o"), (), (), "const", slow=True)
    op(DVE, lambda e: e.memset(ones_bf, 1.0), w=["ones_bf"])
    PS_HALF = True
    op(DVE, lambda e: e.memset(epsc[:, 0:1], LN_EPS), w=["epsc"])
    op(DVE, lambda e: e.memset(epsc[:, 1:2], LN_EPS / (ALPHA * ALPHA)), w=["epsc"])
    op(DVE, lambda e: e.memset(epsc[:, 2:3], QK_EPS), w=["epsc"])
    op(DVE, lambda e: e.memset(epsc[:, 3:4], 0.0), w=["epsc"])
    pb.full_barrier()

    op(DVE, lambda e: e.tensor_scalar(out=psT, in0=psT, scalar1=0.5, scalar2=None, op0=ALU.mult), w=["psT"])
    pb.full_barrier()

    cg_stack = ExitStack()

    def conv_gen():
        ph = cg_stack
        if True:
            stg = Ring("stg", [sb("stg%d" % i, [128, INW], F32, ph) for i in range(2)])
            stb = Ring("stb", [sb("stb%d" % i, [128, INW + 256], BF16, ph) for i in range(2)])
            cnt = 0
            pwfs = [sb("pwf%d" % l, [128, 2, 128], F32, ph) for l in range(2)]
            for l in range(2):
                pwf = pwfs[l]
                op(DVE, lambda e: e.memset(pwf, 0.0), w=[("pwf", l)])
                for g in range(4):
                    ci, hf = g // 2, g % 2
                    dma(SP, pwf[hf * 64:(hf + 1) * 64, ci, hf * 64:(hf + 1) * 64], poolw_d[l, g], (), [("pwf", l)], "pw")
                op(DVE, lambda e: e.tensor_copy(out=pwblk[:, l, :, :], in_=pwf), r=[("pwf", l)], w=["pwblk"])
                for k in range(KC):
                    st, sk, ss = stg.next(); bt, bk, bs = stb.next()
                    dma(SP, st, win_d[l, k * 128:(k + 1) * 128, :], (), [sk], ss)
                    eng = DVE if cnt % 2 == 0 else ACT
                    cnt += 1
                    if eng == DVE:
                        op(DVE, lambda e: e.tensor_copy(out=bt[:, 0:INW], in_=st), r=[sk], w=[bk])
                        op(DVE, lambda e: e.tensor_copy(out=bt[:, INW:INW + 256].rearrange("p (h r d) -> p h r d", h=2, r=2),
                                                        in_=st[:, 2048:2176].rearrange("p (h d) -> p h d", h=2).unsqueeze(2).to_broadcast([128, 2, 2, 64])),
                           r=[sk], w=[bk])
                    else:
                        op(ACT, lambda e: e.copy(out=bt[:, 0:INW], in_=st), r=[sk], w=[bk])
                        op(ACT, lambda e: e.copy(out=bt[:, INW:INW + 256].rearrange("p (h r d) -> p h r d", h=2, r=2),
                                                 in_=st[:, 2048:2176].rearrange("p (h d) -> p h d", h=2).unsqueeze(2).to_broadcast([128, 2, 2, 64])),
                           r=[sk], w=[bk])
                    dma(POOL, winbf_d[l, :, :, k, :].rearrange("m p c -> p m c"), bt.rearrange("p (m c) -> p m c", c=128), [bk], [], bs)
                    yield
                for k in range(KC):
                    st, sk, ss = stg.next(); bt, bk, bs = stb.next()
                    dma(SP, st[:, 0:D], wout_d[l, k * 128:(k + 1) * 128, :], (), [sk], ss)
                    op(DVE, lambda e: e.tensor_copy(out=bt[:, 0:D], in_=st[:, 0:D]), r=[sk], w=[bk])
                    dma(POOL, woutbf_d[l, :, k, :], bt[:, 0:D], [bk], [], bs)
                    yield


    cg = conv_gen()
    next(cg, None)

    with ExitStack() as ph:
        scT = sb("scT", [128, KC, NS], F32, ph)
        badaT = sb("badaT", [128, 2, 24], F32, ph)
        wab = Ring("wab", [sb("wab%d" % i, [128, KC, 512], F32, ph) for i in range(2)])
        for b in range(NB):
            dma(SP, scT[:, :, b:b + 1], c_d[b:b + 1, :].rearrange("o (k p) -> p k o", p=128), (), ["scT"], "m_sc", slow=True)
        dma(SP, scT[:, :, NB:NB + 1], cctx_d.rearrange("o (k p) -> p k o", p=128), (), ["scT"], "m_sc", slow=True)
        for l in range(2):
            dma(SP, badaT[:, l, :], bada_d[l:l + 1, :].rearrange("o (j p) -> p (o j)", p=128), (), ["badaT"], "m_ba", slow=True)
        op(ACT, lambda e: e.activation(out=scT, in_=scT, func=AF.Silu), r=["scT"], w=["scT"])
        op(DVE, lambda e: e.tensor_scalar(out=badaT[:, :, 8:16], in0=badaT[:, :, 8:16], scalar1=1.0, scalar2=None, op0=ALU.add),
           r=["badaT"], w=["badaT"])
        for l in range(2):
            for jb in range(6):
                wt, wk, ws = wab.next()
                dma(SP, wt, wada_d[l, :, jb * 512:(jb + 1) * 512].rearrange("(k p) n -> p k n", p=128), (), [wk], ws)
                for jj in range(4):
                    j = jb * 4 + jj
                    pt, pk = psr.next()
                    for k in range(KC):
                        op(PE, lambda e: e.matmul(pt[:, 0:NS], lhsT=wt[:, k, jj * 128:(jj + 1) * 128], rhs=scT[:, k, :], start=(k == 0), stop=(k == KC - 1)),
                           r=[wk, "scT"], w=[pk], sig=(k == KC - 1))
                    op(DVE, lambda e: e.tensor_scalar(out=modT[:, l, j, :], in0=pt[:, 0:NS], scalar1=badaT[:, l, j:j + 1], scalar2=None, op0=ALU.add),
                       r=[pk, "badaT"], w=["modT"])
        dump("modT", modT, ["modT"])
        pb.barrier()

    def hyena_filter(l, nm, Lh):
        TC = Lh // 128; NJ = 2 * TC; NP_ = TC; BWf = min(512, Lh); NBk = Lh // BWf
        with ExitStack() as ph:
            zT = sb("zT", [33, Lh], F32, ph)
            w1 = sb("w1", [33, 64], F32, ph); w2 = sb("w2", [64, 64], F32, ph); w3 = sb("w3", [64, 512], F32, ph)
            prm = sb("prm", [64, 8], F32, ph)
            h1 = sb("h1", [64, Lh], F32, ph); h2 = sb("h2", [64, Lh], F32, ph)
            ty = sb("ty", [64, 512], F32, ph); tr = sb("tr", [64, 512], F32, ph)
            hn = sb("hn", [128, TC, 512], BF16, ph)
            hdr = Ring("hdr", [sb("hdr%d" % i, [128, 512], F32, ph) for i in range(2)])
            dcr = Ring("dcr", [sb("dcr%d" % i, [128, 512], F32, ph) for i in range(2)])
            ha = sb("ha", [128, 512], F32, ph); nrm = sb("nrm", [128, 256], F32, ph); rinv = sb("rinv", [128, 256], F32, ph)
            qs = sb("qs", [128, 256], F32, ph); kt = Ring("kt", [sb("kt%d" % i, [128, 256], F32, ph) for i in range(2)])
            wfr = Ring("wff", [sb("wff%d" % i, [128, TC, 128], BF16, ph) for i in range(3)])
            ld = [(zT, cd["zT_" + nm]), (w1, hw1_d[l]), (w2, hw2_d[l]), (w3, hw3_d[l])]
            for dst, src in ld:
                dma(SP, dst, src, (), ["hf_in"], "hfl")
            dma(SP, prm[:, 0:1], hb1_d[l:l + 1, :].rearrange("o d -> d o"), (), ["hf_in"], "hfl", slow=True)
            dma(SP, prm[:, 1:2], hb2_d[l:l + 1, :].rearrange("o d -> d o"), (), ["hf_in"], "hfl", slow=True)
            dma(SP, prm[:, 2:4], hfq_d[l].rearrange("t d -> d t"), (), ["hf_in"], "hfl", slow=True)
            hs = pb.sems["hfl"]
            inv2pi = 1.0 / (2.0 * math.pi)
            op(DVE, lambda e: e.tensor_scalar(out=prm[:, 4:5], in0=prm[:, 2:3], scalar1=inv2pi, scalar2=None, op0=ALU.mult), r=["hf_in"], w=["prm"])
            op(DVE, lambda e: e.tensor_scalar(out=prm[:, 6:7], in0=prm[:, 3:4], scalar1=inv2pi, scalar2=None, op0=ALU.mult), w=["prm"])
            op(DVE, lambda e: e.tensor_tensor(out=prm[:, 5:6], in0=prm[:, 4:5], in1=prm[:, 0:1], op=ALU.mult), w=["prm"])
            op(DVE, lambda e: e.tensor_tensor(out=prm[:, 7:8], in0=prm[:, 6:7], in1=prm[:, 1:2], op=ALU.mult), w=["prm"])
            op(PE, lambda e: e.wait_ge(hs[0], hs[1]), r=["hf_in"], sig=False)

            def sin_layer(src_w, src_rhs, kdim, acol, bcol, dst):
                for n in range(NBk):
                    pt, pk = psr.next()
                    op(PE, lambda e: e.matmul(pt[0:64, 0:BWf], lhsT=src_w, rhs=src_rhs[0:kdim, n * BWf:(n + 1) * BWf], start=True, stop=True),
                       r=["prm", ("hsrc", kdim)], w=[pk])
                    op(DVE, lambda e: e.tensor_scalar(out=ty[:, 0:BWf], in0=pt[0:64, 0:BWf], scalar1=prm[:, acol:acol + 1], scalar2=prm[:, bcol:bcol + 1], op0=ALU.mult, op1=ALU.add),
                       r=[pk, "prm"], w=["ty"])
                    op(DVE, lambda e: e.tensor_scalar(out=tr[:, 0:BWf], in0=ty[:, 0:BWf], scalar1=MAGIC, scalar2=None, op0=ALU.add), r=["ty"], w=["tr"])
                    op(DVE, lambda e: e.tensor_scalar(out=tr[:, 0:BWf], in0=tr[:, 0:BWf], scalar1=MAGIC, scalar2=None, op0=ALU.subtract), r=["tr"], w=["tr"])
                    op(DVE, lambda e: e.tensor_tensor(out=ty[:, 0:BWf], in0=ty[:, 0:BWf], in1=tr[:, 0:BWf], op=ALU.subtract), r=["ty", "tr"], w=["ty"])
                    op(ACT, lambda e: e.activation(out=dst[:, n * BWf:(n + 1) * BWf], in_=ty[:, 0:BWf], func=AF.Sin, scale=6.283185),
                       r=["ty"], w=[("hsrc", 64)])

            sin_layer(w1, zT, 33, 4, 5, h1)
            sin_layer(w2, h1, 64, 6, 7, h2)
            st_, sk_ = psr.next()
            for tc in range(TC):
                pt, pk = psr.next(0, 6)
                op(PE, lambda e: e.matmul(pt, lhsT=h2[:, tc * 128:(tc + 1) * 128], rhs=w3, start=True, stop=True), r=[("hsrc", 64)], w=[pk])
                dt_, dtk, dts = dcr.next()
                dma(SP, dt_, cd["dec_" + nm][:, tc, :], (), [dtk], dts)
                hd, hdk, _ = hdr.next()
                op(DVE, lambda e: e.tensor_tensor(out=hd, in0=pt, in1=dt_, op=ALU.mult), r=[pk, dtk], w=[hdk])
                op(ACT, lambda e: e.activation(out=ha, in_=hd, func=AF.Abs), r=[hdk], w=["ha"])
                op(DVE, lambda e: e.tensor_copy(out=hn[:, tc, :], in_=hd), r=[hdk], w=["hn"])
                op(PE, lambda e: e.matmul(psum[7], lhsT=ones_f, rhs=ha, start=(tc == 0), stop=(tc == TC - 1)), r=["ha"], w=[("ps", 7)])
            op(ACT, lambda e: e.copy(out=qs, in_=psum[7][:, 256:512]), r=[("ps", 7)], w=["qs"])
            op(DVE, lambda e: e.tensor_tensor(out=nrm, in0=psum[7][:, 0:256], in1=qs, op=ALU.add), r=[("ps", 7), "qs"], w=["nrm"])
            op(DVE, lambda e: e.reciprocal(out=rinv, in_=nrm), r=["nrm"], w=["rinv"])
            for j in range(NJ):
                next(cg, None)
                wt, wk, ws = wfr.next()
                dma(SP, wt, cd["Wf_" + nm][j], (), [wk], ws)
                pt, pk = psr.next(0, 6)
                for tc in range(TC):
                    op(PE, lambda e: e.matmul(pt, lhsT=wt[:, tc, :], rhs=hn[:, tc, :], start=(tc == 0), stop=(tc == TC - 1)), r=[wk, "hn"], w=[pk], sig=(tc == TC - 1))
                op(ACT, lambda e: e.copy(out=qs, in_=pt[:, 256:512]), r=[pk], w=["qs"])
                jp, comp = j % NP_, j // NP_
                dst, ktk, kts = kt.next(); dk = ktk
                op(DVE, lambda e: e.tensor_tensor(out=dst, in0=pt[:, 0:256], in1=qs, op=(ALU.add if comp == 0 else ALU.subtract)), r=[pk, "qs"], w=[dk])
                if comp == 1 and jp == 0:
                    op(DVE, lambda e: e.tensor_tensor(out=dst[0:1, :], in0=pt[0:1, 0:256], in1=qs[0:1, :], op=ALU.add), r=[pk, "qs"], w=[dk])
                op(DVE, lambda e: e.tensor_tensor(out=dst, in0=dst, in1=rinv, op=ALU.mult), r=[dk, "rinv"], w=[dk])
                dma(POOL, (kh_d[l] if nm == "x" else khc_d)[jp, :, comp, :], dst, [dk], [("kh", l)], kts)
        pb.full_barrier()

    hyena_filter(0, "x", L)
    hyena_filter(1, "x", L)
    hyena_filter(0, "c", LC)
    for _ in cg:
        pass
    pb.full_barrier()
    cg_stack.close()

    hxT = sb("hxT", [128, KC, L], BF16)
    cat = sb("cat", [128, KC, L], BF16)
    kTc = sb("kTc", [128, 2, LC], BF16)
    vcx = sb("vcx", [128, 2, 128], BF16)
    wring = Ring("wi", [sb("wslot%d" % i, [128, 2, KC, 128], BF16) for i in range(2)])

    def load_w(l, m0, n=2):
        wt, wk, ws = wring.next()
        dma(SP, wt[:, 0:n, :, :], winbf_d[l, m0:m0 + n].rearrange("m p k c -> p m k c"), (), [wk], ws)
        return wt, wk

    def proj(wt, wk, mi, n, BW, ps_lo=0, ps_hi=8):
        pt, pk = psr.next(ps_lo, ps_hi)
        for k in range(KC):
            op(PE, lambda e: e.matmul(pt[:, 0:BW], lhsT=wt[:, mi, k, :], rhs=hxT[:, k, n * BW:(n + 1) * BW], start=(k == 0), stop=(k == KC - 1)),
               r=[wk, "hxT"], w=[pk], sig=(k == KC - 1))
        return pt, pk

    def seq_block(l, b, Ls, src_d, dst_d, is_ctx, kv_only, dkey_src, dkey_dst):
        NT = Ls // 128; BW = min(512, Ls); NBk = Ls // BW
        col = NB if is_ctx else b
        last = (l == DEPTH - 1)
        G = min(4, NT); NG = NT // G
        psb = psum_all.bitcast(BF16).rearrange("p (b k c) -> p b k c", b=8, k=8)
        with ExitStack() as ph:
            xg = [sb("A_x%d" % i, [128, G, D], F32, ph) for i in range(2)]
            xng = [sb("A_xn%d" % i, [128, G, D], BF16, ph) for i in range(2)]
            sts = [sb("A_st%d" % i, [128, G, 12], F32, ph) for i in range(3)]
            mvs = [sb("A_mv%d" % i, [128, G, 2], F32, ph) for i in range(3)]
            rss = [sb("A_rs%d" % i, [128, G, 2], F32, ph) for i in range(3)]

            def a1_(g):
                xt = xg[g % 2]; st = sts[g % 3]; mv = mvs[g % 3]
                dma(SP, xt, src_d[g * G * 128:(g + 1) * G * 128, :].rearrange("(a p) d -> p a d", p=128), [(dkey_src, g * G + a) for a in range(G)], [("A_x", g % 2)], "A_x%d" % (g % 2))
                for a in range(G):
                    for hh in range(2):
                        op(DVE, lambda e: e.bn_stats(out=st[:, a, hh * 6:(hh + 1) * 6], in_=xt[:, a, hh * 512:(hh + 1) * 512]), r=[("A_x", g % 2)], w=[("A_st", g % 3)])
                    op(DVE, lambda e: e.bn_aggr(out=mv[:, a, :], in_=st[:, a, :]), r=[("A_st", g % 3)], w=[("A_mv", g % 3)])

            def a2_(g):
                mv = mvs[g % 3]; rs = rss[g % 3]
                op(ACT, lambda e: e.activation(out=rs[:, :, 0], in_=mv[:, :, 1], func=AF.Sqrt, bias=epsc[:, 0:1], scale=1.0), r=[("A_mv", g % 3)], w=[("A_rs0", g % 3)])

            def a3_(g):
                xt = xg[g % 2]; mv = mvs[g % 3]; rs = rss[g % 3]; xn_ = xng[g % 2]
                op(DVE, lambda e: e.reciprocal(out=rs[:, :, 1], in_=rs[:, :, 0]), r=[("A_rs0", g % 3)], w=[("A_rs1", g % 3)])
                for a in range(G):
                    op(DVE, lambda e: e.tensor_scalar(out=xn_[:, a, :], in0=xt[:, a, :], scalar1=mv[:, a, 0:1], scalar2=rs[:, a, 1:2], op0=ALU.subtract, op1=ALU.mult),
                       r=[("A_x", g % 2), ("A_mv", g % 3), ("A_rs1", g % 3)], w=[("A_xn", g % 2)])

            def a4_(g):
                xn_ = xng[g % 2]
                b0 = (g % 2) * 4
                for a in range(G):
                    for k in range(KC):
                        op(PE, lambda e: e.transpose(psb[:, b0 + a, k, :], xn_[:, a, k * 128:(k + 1) * 128], ident_bf), r=[("A_xn", g % 2)], w=[("ps", b0 + a)], sig=(k == KC - 1))

            def a5_(g):
                b0 = (g % 2) * 4
                for k in range(KC):
                    dst = hxT[:, k, g * G * 128:(g + 1) * G * 128].rearrange("p (a c) -> p a c", c=128)
                    src = psb[:, b0:b0 + G, k, :]
                    rk = [("ps", b0 + a) for a in range(G)]
                    if k % 2 == 0:
                        op(ACT, lambda e: e.activation(out=dst, in_=src, func=AF.Identity, bias=modT[:, l, k, col:col + 1], scale=modT[:, l, 8 + k, col:col + 1]), r=rk, w=["hxT"])
                    else:
                        op(DVE, lambda e: e.tensor_scalar(out=dst, in0=src, scalar1=modT[:, l, 8 + k, col:col + 1], scalar2=modT[:, l, k, col:col + 1], op0=ALU.mult, op1=ALU.add), r=rk, w=["hxT"])

            pipeline(NG, [a1_, a2_, a3_, a4_, a5_])
        dump("hxT_%d_%d_%d" % (l, b, int(is_ctx)), hxT[:, :, 0:Ls], ["hxT"])
        pb.barrier()

        NK = Ls if is_ctx else Ls + LC
        NKC = NK // 128
        OLO, OHI = 6, 8
        ALO, AHI = 2, 6

        def normrope(pt, pk, gcol, dst, dkey, n, rope, tmp, plo, phi):
            sq, qg, rstd, t1, t2 = tmp
            op(ACT, lambda e: e.activation(out=sq[:, 0:BW], in_=pt[:, 0:BW], func=AF.Square), r=[pk], w=["nr_sq"])
            op(ACT, lambda e: e.activation(out=qg[:, 0:BW], in_=pt[:, 0:BW], func=AF.Identity, scale=gcol), r=[pk], w=["nr_qg"])
            p2, k2 = psr.next(plo, phi)
            op(PE, lambda e: e.matmul(p2[:, 0:BW], lhsT=bones, rhs=sq[:, 0:BW], start=True, stop=True), r=["nr_sq"], w=[k2])
            op(ACT, lambda e: e.activation(out=rstd[:, 0:BW], in_=p2[:, 0:BW], func=AF.Ln, bias=epsc[:, 2:3], scale=1.0 / 64), r=[k2], w=["nr_rstd"])
            op(ACT, lambda e: e.activation(out=rstd[:, 0:BW], in_=rstd[:, 0:BW], func=AF.Exp, scale=-0.5), r=["nr_rstd"], w=["nr_rstd"])
            if rope:
                p3, k3 = psr.next(plo, phi)
                op(PE, lambda e: e.matmul(p3[:, 0:BW], lhsT=rotP, rhs=qg[:, 0:BW], start=True, stop=True), r=["nr_qg"], w=[k3])
                op(DVE, lambda e: e.tensor_tensor(out=t1[:, 0:BW], in0=qg[:, 0:BW], in1=ropeC[:, n * BW:(n + 1) * BW], op=ALU.mult), r=["nr_qg"], w=["nr_t1"])
                op(DVE, lambda e: e.tensor_tensor(out=t2[:, 0:BW], in0=p3[:, 0:BW], in1=ropeS[:, n * BW:(n + 1) * BW], op=ALU.mult), r=[k3], w=["nr_t2"])
                op(DVE, lambda e: e.tensor_tensor(out=t1[:, 0:BW], in0=t1[:, 0:BW], in1=t2[:, 0:BW], op=ALU.add), r=["nr_t1", "nr_t2"], w=["nr_t1"])
                op(DVE, lambda e: e.tensor_tensor(out=dst, in0=t1[:, 0:BW], in1=rstd[:, 0:BW], op=ALU.mult), r=["nr_t1", "nr_rstd"], w=[dkey])
            else:
                op(DVE, lambda e: e.tensor_tensor(out=dst, in0=qg[:, 0:BW], in1=rstd[:, 0:BW], op=ALU.mult), r=["nr_qg", "nr_rstd"], w=[dkey])

        def silu2(pt, pk, dst, dkey, th, thk):
            op(ACT, lambda e: e.activation(out=th[:, 0:BW], in_=pt[:, 0:BW], func=AF.Tanh, scale=0.5), r=[pk], w=[thk])
            op(DVE, lambda e: e.scalar_tensor_tensor(out=dst, in0=th[:, 0:BW], scalar=1.0, in1=pt[:, 0:BW], op0=ALU.add, op1=ALU.mult), r=[thk, pk], w=[dkey])

        def others():
            with ExitStack() as ph:
                vp = sb("vp", [128, Ls + 16], F32, ph)
                sA = sb("sA", [128, Ls + 16], F32, ph); sB = sb("sB", [128, Ls + 16], F32, ph)
                dff = sb("dff", [128, Ls], BF16, ph)
                sgr = Ring("sgp", [sb("sgp%d" % i, [128, 512], BF16, ph) for i in range(2)])
                thr = Ring("thp", [sb("thp%d" % i, [128, 512], BF16, ph) for i in range(2)])
                op(POOL, lambda e: e.memset(vp[:, 0:8], 0.0), w=["vp"])
                op(POOL, lambda e: e.memset(vp[:, Ls + 8:Ls + 16], 0.0), w=["vp"])
                wt, wk = load_w(l, 0, 2)
                wg, wgk = load_w(l, 2, 2)
                for ci in range(2):
                    for n in range(NBk):
                        pt, pk = proj(wt, wk, ci, n, BW, OLO, OHI)
                        op(DVE, lambda e: e.tensor_copy(out=vp[:, 8 + n * BW:8 + (n + 1) * BW], in_=pt[:, 0:BW]), r=[pk], w=["vp"])
                        yield
                    W = Ls + 16
                    op(POOL, lambda e: e.tensor_tensor(out=sA[:, 1:W], in0=vp[:, 0:W - 1], in1=vp[:, 1:W], op=ALU.add), r=["vp"], w=["sA"])
                    if ci == 0:
                        op(POOL, lambda e: e.tensor_tensor(out=sB[64:128, 2:W - 1], in0=sA[64:128, 1:W - 2], in1=sA[64:128, 3:W], op=ALU.add), r=["sA"], w=["sB"])
                        halves = [(0, sA, "sA", 2), (1, sB, "sB", 4)]
                    else:
                        op(POOL, lambda e: e.tensor_tensor(out=sB[:, 2:W - 1], in0=sA[:, 1:W - 2], in1=sA[:, 3:W], op=ALU.add), r=["sA"], w=["sB"])
                        op(POOL, lambda e: e.tensor_tensor(out=sA[:, 4:W - 3], in0=sB[:, 2:W - 5], in1=sB[:, 6:W - 1], op=ALU.add), r=["sB"], w=["sA"])
                        op(POOL, lambda e: e.tensor_tensor(out=sB[64:128, 8:W - 7], in0=sA[64:128, 4:W - 11], in1=sA[64:128, 12:W - 3], op=ALU.add), r=["sA"], w=["sB"])
                        halves = [(0, sA, "sA", 8), (1, sB, "sB", 16)]
                    yield
                    for hf, st_, stk, w_ in halves:
                        rows = slice(hf * 64, (hf + 1) * 64)
                        op(POOL, lambda e: e.tensor_tensor(out=st_[rows, 8:16], in0=st_[rows, 8:16], in1=pcorr[rows, ci, 0:8], op=ALU.mult), r=[stk], w=[stk])
                        op(POOL, lambda e: e.tensor_tensor(out=st_[rows, Ls:Ls + 8], in0=st_[rows, Ls:Ls + 8], in1=pcorr[rows, ci, 8:16], op=ALU.mult), r=[stk], w=[stk])
                        yield
                        op(DVE, lambda e: e.scalar_tensor_tensor(out=dff[rows, :], in0=st_[rows, 8:Ls + 8], scalar=1.0 / w_, in1=vp[rows, 8:Ls + 8], op0=ALU.mult, op1=ALU.subtract),
                           r=[stk, "vp"], w=["dff"])
                        yield
                    for n in range(NBk):
                        gp_, gk_ = proj(wg, wgk, ci, n, BW, OLO, OHI)
                        sg, sgk, _ = sgr.next(); th, thk, _ = thr.next()
                        silu2(gp_, gk_, sg[:, 0:BW], sgk, th, thk)
                        yield
                        yp, yk = psr.next(OLO, OHI)
                        op(PE, lambda e: e.matmul(yp[:, 0:BW], lhsT=pwblk[:, l, ci, :], rhs=dff[:, n * BW:(n + 1) * BW], start=True, stop=True), r=["dff", "pwblk"], w=[yk])
                        yield
                        op(DVE, lambda e: e.scalar_tensor_tensor(out=cat[:, ci, n * BW:(n + 1) * BW], in0=yp[:, 0:BW], scalar=psT[:, l, ci:ci + 1], in1=sg[:, 0:BW], op0=ALU.mult, op1=ALU.mult),
                           r=[yk, sgk], w=["cat"])
            pb.barrier()
            yield
            with ExitStack() as phu:
                u_bf = sb("u_bf", [128, 2, Ls], BF16, phu); g0 = sb("g0", [128, 2, Ls], BF16, phu)
                with ExitStack() as ph:
                    pin = sb("pin", [128, Ls + 2], F32, ph)
                    coA = sb("coA", [128, Ls], F32, ph); coB = sb("coB", [128, Ls], F32, ph)
                    sgt = sb("sgt", [128, Ls], BF16, ph)
                    thr = Ring("thh", [sb("thh%d" % i, [128, 512], BF16, ph) for i in range(2)])
                    op(POOL, lambda e: e.memset(pin[:, 0:1], 0.0), w=["pin"])
                    op(POOL, lambda e: e.memset(pin[:, Ls + 1:Ls + 2], 0.0), w=["pin"])

                    def conv(wt, wk, mi, mm, co, cok):
                        for n in range(NBk):
                            pt, pk = proj(wt, wk, mi, n, BW, OLO, OHI)
                            op(DVE, lambda e: e.tensor_copy(out=pin[:, 1 + n * BW:1 + (n + 1) * BW], in_=pt[:, 0:BW]), r=[pk], w=["pin"])
                            yield
                        op(DVE, lambda e: e.tensor_scalar(out=co, in0=pin[:, 1:Ls + 1], scalar1=cwT[:, l, mm, 1:2], scalar2=cbT[:, l, mm:mm + 1], op0=ALU.mult, op1=ALU.add), r=["pin"], w=[cok])
                        yield
                        op(DVE, lambda e: e.scalar_tensor_tensor(out=co, in0=pin[:, 0:Ls], scalar=cwT[:, l, mm, 0:1], in1=co, op0=ALU.mult, op1=ALU.add), r=["pin", cok], w=[cok])
                        yield
                        op(DVE, lambda e: e.scalar_tensor_tensor(out=co, in0=pin[:, 2:Ls + 2], scalar=cwT[:, l, mm, 2:3], in1=co, op0=ALU.mult, op1=ALU.add), r=["pin", cok], w=[cok])
                        yield

                    w_x1, k_x1 = load_w(l, 6, 2)
                    w_v, k_v = load_w(l, 8, 2)
                    for ci in range(2):
                        yield from conv(w_x1, k_x1, ci, 2 + ci, coA, "coA")
                        yield from conv(w_v, k_v, ci, 4 + ci, coB, "coB")
                        op(DVE, lambda e: e.tensor_tensor(out=u_bf[:, ci, 0:Ls], in0=coA, in1=coB, op=ALU.mult), r=["coA", "coB"], w=["u_bf"])
                        yield
                    w_x0, k_x0 = load_w(l, 4, 2)
                    w_g, k_g = load_w(l, 10, 2)
                    for ci in range(2):
                        yield from conv(w_x0, k_x0, ci, ci, coA, "coA")
                        for n in range(NBk):
                            pt, pk = proj(w_g, k_g, ci, n, BW, OLO, OHI)
                            th, thk, _ = thr.next()
                            silu2(pt, pk, sgt[:, n * BW:(n + 1) * BW], "sgt", th, thk)
                            yield
                        op(DVE, lambda e: e.scalar_tensor_tensor(out=g0[:, ci, 0:Ls], in0=coA, scalar=0.5, in1=sgt, op0=ALU.mult, op1=ALU.mult), r=["coA", "sgt"], w=["g0"])
                        yield
                dump("ubf_%d_%d_%d" % (l, b, int(is_ctx)), u_bf[:, :, 0:Ls], ["u_bf"])
                dump("g0_%d_%d_%d" % (l, b, int(is_ctx)), g0[:, :, 0:Ls], ["g0"])
                pb.barrier()
                yield
                if VERBOSE:
                    print("H2 start step", ilc[0], "Ls", Ls)
                TC = NT; NP_ = NT; NJ = 2 * NT
                nmm = "c" if is_ctx else "x"
                with ExitStack() as ph:
                    uT = sb("uT", [128, TC, 256], BF16, ph)
                    Y = sb("Y", [128, NJ, 256], BF16, ph)
                    wfr = Ring("wf", [sb("wf%d" % i, [128, TC, 128], BF16, ph) for i in range(2)])
                    khr = Ring("khr", [sb("khr%d" % i, [128, 2, 256], F32, ph) for i in range(2)])
                    tt = [sb("ht%d" % i, [128, 256], F32, ph) for i in range(4)]
                    accs = [sb("hacc%d" % i, [128, 512], F32, ph) for i in range(2)]
                    if is_ctx:
                        wic = sb("wic", [128, NJ, Ls], BF16, ph)
                        dma(SP, wic, cd["Wi_c"], (), ["wic"], "wic")
                    else:
                        wir = Ring("wir", [sb("wir%d" % i, [128, 4, 512], BF16, ph) for i in range(2)])
                    for t0 in range(0, TC, 4):
                        nt4 = min(4, TC - t0)
                        pt, pk = psr.next(OLO, OHI)
                        ptb = pt.bitcast(BF16)
                        for a_ in range(nt4):
                            for ci in range(2):
                                op(PE, lambda e: e.transpose(ptb[:, (a_ * 2 + ci) * 128:(a_ * 2 + ci + 1) * 128], u_bf[:, ci, (t0 + a_) * 128:(t0 + a_ + 1) * 128], ident_bf),
                                   r=["u_bf"], w=[pk], sig=(a_ == nt4 - 1 and ci == 1))
                        yield
                        op(DVE, lambda e: e.tensor_copy(out=uT[:, t0:t0 + nt4, :], in_=ptb[:, 0:nt4 * 256].rearrange("p (a c) -> p a c", c=256)), r=[pk], w=["uT"])
                        yield
                    if VERBOSE:
                        print("H2 fwd start step", ilc[0])
                    for jp in range(NP_):
                        pss = []
                        for comp in range(2):
                            wt, wk, ws = wfr.next()
                            dma(SP, wt, cd["Wf_" + nmm][comp * NP_ + jp], (), [wk], ws)
                            pt, pk = psum[OLO + comp], ("ps", OLO + comp)
                            for tc in range(TC):
                                op(PE, lambda e: e.matmul(pt[:, 0:256], lhsT=wt[:, tc, :], rhs=uT[:, tc, :], start=(tc == 0), stop=(tc == TC - 1)), r=[wk, "uT"], w=[pk], sig=(tc == TC - 1))
                            pss.append((pt, pk))
                            yield
                        kt_, kk, ks = khr.next()
                        dma(SP, kt_, (khc_d if is_ctx else kh_d[l])[jp], [("kh", l, nmm)], [kk], ks)
                        kre, kim = kt_[:, 0, :], kt_[:, 1, :]
                        (pr, prk), (pi_, pik) = pss
                        op(DVE, lambda e: e.tensor_tensor(out=tt[0], in0=pr[:, 0:256], in1=kre, op=ALU.mult), r=[prk, kk], w=["ht0"])
                        op(DVE, lambda e: e.tensor_tensor(out=tt[1], in0=pi_[:, 0:256], in1=kim, op=ALU.mult), r=[pik, kk], w=["ht1"])
                        op(DVE, lambda e: e.tensor_tensor(out=tt[2], in0=pr[:, 0:256], in1=kim, op=ALU.mult), r=[prk, kk], w=["ht2"])
                        op(DVE, lambda e: e.tensor_tensor(out=tt[3], in0=pi_[:, 0:256], in1=kre, op=ALU.mult), r=[pik, kk], w=["ht3"])
                        yield
                        op(DVE, lambda e: e.tensor_tensor(out=Y[:, jp, :], in0=tt[0], in1=tt[1], op=ALU.subtract), r=["ht0", "ht1"], w=["Y"])
                        op(DVE, lambda e: e.tensor_tensor(out=Y[:, NP_ + jp, :], in0=tt[2], in1=tt[3], op=ALU.add), r=["ht2", "ht3"], w=["Y"])
                        if jp == 0:
                            op(DVE, lambda e: e.tensor_copy(out=Y[0:1, 0, :], in_=tt[0][0:1, :]), r=["ht0"], w=["Y"])
                            op(DVE, lambda e: e.tensor_copy(out=Y[0:1, NP_, :], in_=tt[1][0:1, :]), r=["ht1"], w=["Y"])
                        yield
                    if VERBOSE:
                        print("H2 inv start step", ilc[0])
                    for n in range(NBk):
                        ops_ = [(psum[OLO], ("ps", OLO)), (psum[OLO + 1], ("ps", OLO + 1))]
                        if is_ctx:
                            for ci in range(2):
                                for j in range(NJ):
                                    op(PE, lambda e: e.matmul(ops_[ci][0][:, 0:BW], lhsT=Y[:, j, ci * 128:(ci + 1) * 128], rhs=wic[:, j, :], start=(j == 0), stop=(j == NJ - 1)),
                                       r=["Y", "wic"], w=[ops_[ci][1]], sig=(j == NJ - 1))
                                yield
                        else:
                            for g in range(NJ // 4):
                                wt, wk, ws = wir.next()
                                dma(SP, wt, cd["Wi_x"][n, g], (), [wk], ws)
                                for ci in range(2):
                                    for jj in range(4):
                                        j = g * 4 + jj
                                        op(PE, lambda e: e.matmul(ops_[ci][0][:, 0:BW], lhsT=Y[:, j, ci * 128:(ci + 1) * 128], rhs=wt[:, jj, :], start=(jj == 0), stop=(jj == 3)),
                                           r=["Y", wk], w=[ops_[ci][1]], sig=(jj == 3))
                                for ci in range(2):
                                    if g == 0:
                                        op(DVE, lambda e: e.tensor_copy(out=accs[ci][:, 0:BW], in_=ops_[ci][0][:, 0:BW]), r=[ops_[ci][1]], w=[("hacc", ci)])
                                    else:
                                        op(DVE, lambda e: e.tensor_tensor(out=accs[ci][:, 0:BW], in0=accs[ci][:, 0:BW], in1=ops_[ci][0][:, 0:BW], op=ALU.add), r=[ops_[ci][1], ("hacc", ci)], w=[("hacc", ci)])
                                yield
                        for ci in range(2):
                            if is_ctx:
                                op(DVE, lambda e: e.scalar_tensor_tensor(out=accs[ci][:, 0:BW], in0=u_bf[:, ci, n * BW:(n + 1) * BW], scalar=hbT[:, l, ci:ci + 1], in1=ops_[ci][0][:, 0:BW], op0=ALU.mult, op1=ALU.add),
                                   r=["u_bf", ops_[ci][1]], w=[("hacc", ci)])
                            else:
                                op(DVE, lambda e: e.scalar_tensor_tensor(out=accs[ci][:, 0:BW], in0=u_bf[:, ci, n * BW:(n + 1) * BW], scalar=hbT[:, l, ci:ci + 1], in1=accs[ci][:, 0:BW], op0=ALU.mult, op1=ALU.add),
                                   r=["u_bf", ("hacc", ci)], w=[("hacc", ci)])
                            op(DVE, lambda e: e.tensor_tensor(out=cat[:, 2 + ci, n * BW:(n + 1) * BW], in0=accs[ci][:, 0:BW], in1=g0[:, ci, n * BW:(n + 1) * BW], op=ALU.mult), r=[("hacc", ci), "g0"], w=["cat"])
                            yield

        with ExitStack() as ph:
            tmp = (sb("nr_sq", [128, 512], BF16, ph), sb("nr_qg", [128, 512], BF16, ph), sb("nr_rstd", [128, 512], F32, ph),
                   sb("nr_t1", [128, 512], F32, ph), sb("nr_t2", [128, 512], F32, ph))
            awr = Ring("aw", [sb("awslot%d" % i, [128, 2, KC, 128], BF16, ph) for i in range(2)])

            def load_wa(m0):
                wt, wk, ws = awr.next()
                dma(SP, wt, winbf_d[l, m0:m0 + 2].rearrange("m p k c -> p m k c"), (), [wk], ws)
                return wt, wk

            if is_ctx:
                kT = kTc; vE = vcx
            else:
                kT = sb("kT", [128, 2, NK], BF16, ph); vE = sb("vE", [128, NKC, 128], BF16, ph)
            wt, wk = load_wa(22)
            for h in range(2):
                for n in range(NBk):
                    pt, pk = proj(wt, wk, h, n, BW)
                    normrope(pt, pk, gk[:, l:l + 1], kT[:, h, n * BW:(n + 1) * BW], "kT", n, not is_ctx, tmp, 0, 8)
            wt, wk = load_wa(16)
            for i0 in range(0, NT, 4):
                pt, pk = psr.next()
                ni = min(4, NT - i0)
                for ii in range(ni):
                    i = i0 + ii
                    for k in range(KC):
                        op(PE, lambda e: e.matmul(pt[:, ii * 128:(ii + 1) * 128], lhsT=hxT[:, k, i * 128:(i + 1) * 128], rhs=wt[:, 1, k, :], start=(k == 0), stop=(k == KC - 1)),
                           r=[wk, "hxT"], w=[pk], sig=(ii == ni - 1 and k == KC - 1))
                op(ACT, lambda e: e.copy(out=vE[:, i0:i0 + ni, :], in_=pt[:, 0:ni * 128].rearrange("p (a c) -> p a c", c=128)), r=[pk], w=["vE"])
            if not is_ctx:
                op(DVE, lambda e: e.tensor_copy(out=kT[:, :, Ls:Ls + LC], in_=kTc), r=["kT"], w=["kT"])
                op(DVE, lambda e: e.tensor_copy(out=vE[:, NT:NT + 2, :], in_=vcx), r=["vE"], w=["vE"])
            dump("kTc_%d_%d_%d" % (l, b, int(is_ctx)), kTc, ["kT"])
            dump("vcx_%d_%d_%d" % (l, b, int(is_ctx)), vcx, ["vE"])
            if not kv_only:
                qT = sb("qT", [128, Ls], BF16, ph)
                pTr = Ring("pT", [sb("pT%d" % i, [128, 2, 512], BF16, ph) for i in range(3)])
                sga = sb("sga", [128, Ls], BF16, ph)
                tha = sb("tha", [128, 512], BF16, ph)
                osb = sb("osb", [128, 512], F32, ph); dsb = sb("dsb", [128, 512], F32, ph)
                rdn = sb("rdn", [128, 512], F32, ph); ot = sb("ot", [128, 512], F32, ph)
                gen = others()
                ilc = [0]

                def gstep():
                    if ilc[0] < INTERLEAVE:
                        ilc[0] += 1
                        next(gen, None)
                scnt = [0]
                for m in range(4):
                    h = m // 2
                    if m % 2 == 0:
                        wq, wqk = load_wa(12 + m)
                        wg, wgk = load_wa(18 + m)
                    for n in range(NBk):
                        pt, pk = proj(wq, wqk, m % 2, n, BW, ALO, AHI)
                        normrope(pt, pk, gq[:, l:l + 1], qT[:, n * BW:(n + 1) * BW], "qT", n, not is_ctx, tmp, ALO, AHI)
                        gp_, gk_ = proj(wg, wgk, m % 2, n, BW, ALO, AHI)
                        silu2(gp_, gk_, sga[:, n * BW:(n + 1) * BW], "sga", tha, "tha")
                        gstep()

                    def emit_SE(n, kc):
                        pT, pTk, _ = pTr.next()
                        pr_ = scnt[0] % 2
                        scnt[0] += 1
                        b0 = ALO + 2 * pr_
                        for hf in range(2):
                            rows = slice(hf * 64, (hf + 1) * 64)
                            op(PE, lambda e: e.matmul(psum[b0 + hf][:, 0:BW], lhsT=kT[rows, h, kc * 128:(kc + 1) * 128], rhs=qT[rows, n * BW:(n + 1) * BW], start=True, stop=True),
                               r=["kT", "qT"], w=[("ps", b0 + hf)], sig=(hf == 1))
                        op(ACT, lambda e: e.activation(out=pT[:, :, 0:BW], in_=psum_all[:, b0 * 512:(b0 + 2) * 512].rearrange("p (h c) -> p h c", h=2)[:, :, 0:BW], func=AF.Exp, scale=0.125),
                           r=[("ps", b0), ("ps", b0 + 1)], w=[pTk])
                        return pT, pTk

                    def emit_V(n, kc, pT, pTk):
                        o_ps = psum[0]; d_ps = psum[1]
                        okey = ("ps", 0); dkey = ("ps", 1)
                        for hf in range(2):
                            rows = slice(hf * 64, (hf + 1) * 64)
                            op(PE, lambda e: e.matmul(o_ps[rows, 0:BW], lhsT=vE[:, kc, h * 64:(h + 1) * 64], rhs=pT[:, hf, 0:BW], start=(kc == 0), stop=(kc == NKC - 1)),
                               r=["vE", pTk], w=[okey], sig=False)
                        for hf in range(2):
                            rows = slice(hf * 64, (hf + 1) * 64)
                            op(PE, lambda e: e.matmul(d_ps[rows, 0:BW], lhsT=ones_bf, rhs=pT[:, hf, 0:BW], start=(kc == 0), stop=(kc == NKC - 1)),
                               r=["ones_bf", pTk], w=[dkey], sig=(hf == 1))
                        if kc == NKC - 1:
                            op(DVE, lambda e: e.tensor_copy(out=dsb[:, 0:BW], in_=d_ps[:, 0:BW]), r=[dkey], w=["dsb"])
                            op(DVE, lambda e: e.tensor_copy(out=osb[:, 0:BW], in_=o_ps[:, 0:BW]), r=[okey], w=["osb"])
                            op(DVE, lambda e: e.reciprocal(out=rdn[:, 0:BW], in_=dsb[:, 0:BW]), r=["dsb"], w=["rdn"])
                            op(DVE, lambda e: e.tensor_tensor(out=ot[:, 0:BW], in0=osb[:, 0:BW], in1=rdn[:, 0:BW], op=ALU.mult), r=["osb", "rdn"], w=["ot"])
                            op(DVE, lambda e: e.scalar_tensor_tensor(out=cat[:, 4 + m, n * BW:(n + 1) * BW], in0=ot[:, 0:BW], scalar=0.5, in1=sga[:, n * BW:(n + 1) * BW], op0=ALU.mult, op1=ALU.mult), r=["ot", "sga"], w=["cat"])

                    iters = [(n, kc) for n in range(NBk) for kc in range(NKC)]
                    pend = []
                    for t in range(len(iters) + 1):
                        if t < len(iters):
                            n_, kc_ = iters[t]
                            pend.append((n_, kc_) + emit_SE(n_, kc_))
                        if t >= 1:
                            emit_V(*pend[t - 1])
                            gstep()
                for _ in gen:
                    pass
        dump("catH_%d_%d_%d" % (l, b, int(is_ctx)), cat[:, :, 0:Ls], ["cat"])
        pb.barrier()
        if kv_only:
            return

        with ExitStack() as ph:
            gb = sb("gb", [128, KC, 128], F32, ph)
            lgb = sb("lgb", [128, 2, D], F32, ph)
            wout_sb = sb("wout_sb", [128, KC, D], BF16, ph)
            Gp = sb("Gp", [128, D], F32, ph)
            GO = min(2, NT); NGO = NT // GO
            zg = [sb("O_z%d" % i, [128, GO, D], F32, ph) for i in range(4)]
            xo = [sb("O_x%d" % i, [128, GO, D], F32, ph) for i in range(2)]
            sts = [sb("O_st%d" % i, [128, GO, 12], F32, ph) for i in range(3)]
            mvs = [sb("O_mv%d" % i, [128, GO, 2], F32, ph) for i in range(3)]
            rss = [sb("O_rs%d" % i, [128, GO, 3], F32, ph) for i in range(3)]
            dma(SP, wout_sb, woutbf_d[l], (), ["wout"], "wout")
            dma(SP, lgb[:, 0, :], lng_d[l:l + 1, :].partition_broadcast(128), (), ["lgb"], "lgb")
            dma(SP, lgb[:, 1, :], lnb_d[l:l + 1, :].partition_broadcast(128), (), ["lgb"], "lgb")
            op(DVE, lambda e: e.tensor_scalar(out=gb, in0=modT[:, l, 16:24, col:col + 1].to_broadcast([128, KC, 128]), scalar1=1.0 / ALPHA, scalar2=None, op0=ALU.mult), w=["gb"])
            for hh in range(2):
                pt, pk = psum[hh], ("ps", hh)
                for kk_ in range(4):
                    k = hh * 4 + kk_
                    op(PE, lambda e: e.matmul(pt[:, kk_ * 128:(kk_ + 1) * 128], lhsT=gb[:, k, :], rhs=ident_f, start=True, stop=True), r=["gb"], w=[pk])
                op(ACT, lambda e: e.copy(out=Gp[:, hh * 512:(hh + 1) * 512], in_=pt), r=[pk], w=["Gp"])

            def o1_(g):
                xt = xo[g % 2]; z = zg[g % 4]; st = sts[g % 3]; mv = mvs[g % 3]
                dma(SP, xt, src_d[g * GO * 128:(g + 1) * GO * 128, :].rearrange("(a p) d -> p a d", p=128), [(dkey_src, g * GO + a) for a in range(GO)], [("O_x", g % 2)], "O_x%d" % (g % 2))
                b0 = (g % 2) * 4
                for a in range(GO):
                    i = g * GO + a
                    for hh in range(2):
                        bk = b0 + a * 2 + hh
                        for k in range(KC):
                            op(PE, lambda e: e.matmul(psum[bk], lhsT=cat[:, k, i * 128:(i + 1) * 128], rhs=wout_sb[:, k, hh * 512:(hh + 1) * 512], start=(k == 0), stop=(k == KC - 1)),
                               r=["cat", "wout"], w=[("ps", bk)], sig=(k == KC - 1))
                    bb = b0 + a * 2
                    op(DVE, lambda e: e.tensor_tensor(out=z[:, a, :], in0=psum_all[:, bb * 512:(bb + 2) * 512], in1=Gp, op=ALU.mult), r=[("ps", bb), ("ps", bb + 1), "Gp"], w=[("O_z", g % 4)])
                op(DVE, lambda e: e.tensor_tensor(out=z, in0=z, in1=xt, op=ALU.add), r=[("O_z", g % 4), ("O_x", g % 2)], w=[("O_z", g % 4)])
                for a in range(GO):
                    for hh in range(2):
                        op(DVE, lambda e: e.bn_stats(out=st[:, a, hh * 6:(hh + 1) * 6], in_=z[:, a, hh * 512:(hh + 1) * 512]), r=[("O_z", g % 4)], w=[("O_st", g % 3)])
                    op(DVE, lambda e: e.bn_aggr(out=mv[:, a, :], in_=st[:, a, :]), r=[("O_st", g % 3)], w=[("O_mv", g % 3)])

            def o2_(g):
                mv = mvs[g % 3]; rs = rss[g % 3]
                op(ACT, lambda e: e.activation(out=rs[:, :, 0], in_=mv[:, :, 1], func=AF.Sqrt, bias=epsc[:, 1:2], scale=1.0), r=[("O_mv", g % 3)], w=[("O_rs0", g % 3)])

            def o3_(g):
                mv = mvs[g % 3]; rs = rss[g % 3]
                op(DVE, lambda e: e.reciprocal(out=rs[:, :, 1], in_=rs[:, :, 0]), r=[("O_rs0", g % 3)], w=[("O_rs1", g % 3)])
                op(DVE, lambda e: e.scalar_tensor_tensor(out=rs[:, :, 2], in0=mv[:, :, 0], scalar=-1.0, in1=rs[:, :, 1], op0=ALU.mult, op1=ALU.mult), r=[("O_mv", g % 3), ("O_rs1", g % 3)], w=[("O_rs2", g % 3)])

            def o4_(g):
                z = zg[g % 4]; rs = rss[g % 3]
                for a in range(GO):
                    op(ACT, lambda e: e.activation(out=z[:, a, :], in_=z[:, a, :], func=AF.Identity, bias=rs[:, a, 2:3], scale=rs[:, a, 1:2]), r=[("O_z", g % 4), ("O_rs1", g % 3), ("O_rs2", g % 3)], w=[("O_z", g % 4)])

            def o5_(g):
                z = zg[g % 4]
                for a in range(GO):
                    op(POOL, lambda e: e.tensor_tensor(out=z[:, a, :], in0=z[:, a, :], in1=lgb[:, 0, :], op=ALU.mult), r=[("O_z", g % 4), "lgb"], w=[("O_z", g % 4)])
                    op(POOL, lambda e: e.tensor_tensor(out=z[:, a, :], in0=z[:, a, :], in1=lgb[:, 1, :], op=ALU.add), r=[("O_z", g % 4), "lgb"], w=[("O_z", g % 4)])
                dma(POOL, dst_d[g * GO * 128:(g + 1) * GO * 128, :].rearrange("(a p) d -> p a d", p=128), z, [("O_z", g % 4)], [(dkey_dst, g * GO + a) for a in range(GO)], "O_zs%d" % (g % 4))

            pipeline(NGO, [o1_, o2_, o3_, o4_, o5_])
        pb.barrier()

    for b in range(NB):
        seq_block(0, b, LC, ctx_d[b], c1_d, True, False, ("ctxin", b), "c1")
        seq_block(0, b, L, x_d[b], x1_d, False, False, ("xin", b), "x1")
        if dbg and b == 0 and "x1" in dbg:
            pass
        seq_block(1, b, LC, c1_d, None, True, True, "c1", None)
        seq_block(1, b, L, x1_d, out_d[b], False, False, "x1", ("out", b))
    pb.full_barrier()
    es.close()
    return nc, consts


_CACHE = {}


def kernel(**inputs):
    NB = inputs["x"].shape[0] // N_CORES
    if NB not in _CACHE:
        _CACHE[NB] = build(NB)
    nc, consts = _CACHE[NB]
    in_maps = []
    for ci in range(N_CORES):
        sl = slice(ci * NB, (ci + 1) * NB)
        m = {}
        for k, v in inputs.items():
            a = np.asarray(v)
            if k in ("x", "c", "ctx"):
                a = a[sl]
            elif k == "c_ctx":
                a = a.reshape(1, D)
            m[k] = np.ascontiguousarray(a, dtype=np.float32)
        for k, v in consts.items():
            m["k_" + k] = v
        in_maps.append(m)
    res = run_bass_kernel_spmd(nc, in_maps, core_ids=list(range(N_CORES)))
    out = np.concatenate([np.asarray(r["out"]) for r in res.results], axis=0)
    return out.astype(np.float32)
```
